# Optimizing a Trainium2 kernel written in Bass

```python
import jax, jax.numpy as jnp
from jax import lax
import numpy as np

D_MODEL = 2048
BATCH = 8
SEQ = 2048
DEPTH = 4

HEAD_DIM = 128
ATTN_HEADS_PER_GROUP = 4
DILATED_GROUPS = ((128, 1), (512, 4), (2048, 16))
N_ATTN_GROUPS = len(DILATED_GROUPS)
GROUP_WIDTH = ATTN_HEADS_PER_GROUP * HEAD_DIM
ATTN_WIDTH = N_ATTN_GROUPS * GROUP_WIDTH
ATTN_OUT_WIDTH = GROUP_WIDTH
HGRN_HEADS = 8
HGRN_KEY_DIM = 128
HGRN_VAL_DIM = 128
HGRN_WIDTH = HGRN_HEADS * HGRN_KEY_DIM
HGRN_CHUNK = 64
D_FF = 5504
MACARON_WEIGHT = 0.5
EPS = 1e-6
MASK_VALUE = -1e30
SPLIT_SIZES = (ATTN_WIDTH, ATTN_WIDTH, ATTN_WIDTH,
               HGRN_WIDTH, HGRN_WIDTH, HGRN_HEADS * HGRN_VAL_DIM, HGRN_HEADS * HGRN_VAL_DIM,
               2 * D_MODEL)
IN_WIDTH = sum(SPLIT_SIZES)

kernel_name = "hybrid_dilated_attn_hgrn2_macaron"


def rms_norm(x, gain):
    xf = x.astype(jnp.float32)
    y = xf * lax.rsqrt(jnp.mean(xf * xf, axis=-1, keepdims=True) + EPS)
    return (y * gain.astype(jnp.float32)).astype(x.dtype)


def swiglu(h, w_in, w_out):
    gate, up = jnp.split(h @ w_in, 2, axis=-1)
    return (jax.nn.silu(gate) * up) @ w_out


def dilated_window_attention(q, k, v, window, dilation):
    B, S, H, Dh = q.shape
    r = dilation
    nb = window // dilation
    L = S // r
    nblk = -(-L // nb)
    pad = nblk * nb - L

    def to_blocks(t):
        t = t.reshape(B, L, r, H, Dh).transpose(0, 2, 3, 1, 4)
        t = jnp.pad(t, ((0, 0), (0, 0), (0, 0), (0, pad), (0, 0)))
        return t.reshape(B, r, H, nblk, nb, Dh)

    def with_prev(t):
        prev = jnp.concatenate([jnp.zeros_like(t[:, :, :, :1]), t[:, :, :, :-1]], axis=3)
        return jnp.concatenate([prev, t], axis=4)

    qb = to_blocks(q)
    kb = with_prev(to_blocks(k))
    vb = with_prev(to_blocks(v))
    s = jnp.einsum('brhnqd,brhnkd->brhnqk', qb, kb).astype(jnp.float32) * (Dh ** -0.5)
    qi = jnp.arange(nb)[:, None]
    kj = jnp.arange(2 * nb)[None, :]
    dist = nb + qi - kj
    blk = jnp.arange(nblk)[:, None, None]
    valid = (dist >= 0) & (dist <= nb) & ((blk > 0) | (kj >= nb))
    s = jnp.where(valid, s, MASK_VALUE)
    lse = jax.nn.logsumexp(s, axis=-1)
    p = jnp.exp(s - lse[..., None])
    o = jnp.einsum('brhnqk,brhnkd->brhnqd', p.astype(vb.dtype), vb)
    o = o.reshape(B, r, H, nblk * nb, Dh)[:, :, :, :L].transpose(0, 3, 1, 2, 4).reshape(B, S, H, Dh)
    lse = lse.reshape(B, r, H, nblk * nb)[..., :L].transpose(0, 3, 1, 2).reshape(B, S, H)
    return o, lse


def dilated_attention_branch(q, k, v):
    B, S, _ = q.shape
    outs, lses = [], []
    for g, (window, dilation) in enumerate(DILATED_GROUPS):
        sl = slice(g * GROUP_WIDTH, (g + 1) * GROUP_WIDTH)
        qg, kg, vg = (t[..., sl].reshape(B, S, ATTN_HEADS_PER_GROUP, HEAD_DIM) for t in (q, k, v))
        o, lse = dilated_window_attention(qg, kg, vg, window, dilation)
        outs.append(o.astype(jnp.float32))
        lses.append(lse)
    weights = jax.nn.softmax(jnp.stack(lses), axis=0)
    out = jnp.einsum('gbsh,gbshd->bshd', weights, jnp.stack(outs))
    return out.reshape(B, S, ATTN_OUT_WIDTH).astype(q.dtype)


def hgrn2_branch(q, f_logit, i, g_out, lower_bound, norm_gain):
    B, S, _ = q.shape
    H, K, V, C = HGRN_HEADS, HGRN_KEY_DIM, HGRN_VAL_DIM, HGRN_CHUNK
    N = S // C
    dt = q.dtype
    lb = lower_bound.astype(jnp.float32)
    fl = f_logit.astype(jnp.float32)
    log_f = jnp.log(lb + (1.0 - lb) * jax.nn.sigmoid(fl))
    key = (1.0 - lb) * jax.nn.sigmoid(-fl)
    qf = jax.nn.silu(q.astype(jnp.float32))
    vf = i.astype(jnp.float32)

    def chunked(t, w):
        return t.reshape(B, N, C, H, w).transpose(1, 0, 3, 2, 4)

    causal = jnp.tril(jnp.ones((C, C), dtype=bool))

    def step(state, inp):
        qc, kc, vc, lfc = inp
        b = jnp.cumsum(lfc, axis=2)
        o_inter = jnp.einsum('bhtk,bhkv->bhtv', qc * jnp.exp(b), state)
        diff = b[:, :, :, None, :] - b[:, :, None, :, :]
        decay = jnp.exp(jnp.where(causal[:, :, None], diff, MASK_VALUE))
        scores = jnp.einsum('bhtk,bhsk,bhtsk->bhts', qc, kc, decay)
        o = o_inter + jnp.einsum('bhts,bhsv->bhtv', scores, vc)
        b_last = b[:, :, -1:, :]
        state = (jnp.exp(b_last[:, :, 0, :])[..., None] * state
                 + jnp.einsum('bhsk,bhsv->bhkv', kc * jnp.exp(b_last - b), vc))
        return state, o

    s0 = jnp.zeros((B, H, K, V), jnp.float32)
    _, o = lax.scan(step, s0, (chunked(qf, K), chunked(key, K), chunked(vf, V), chunked(log_f, K)))
    o = o.transpose(1, 0, 3, 2, 4).reshape(B, S, H, V)
    o = o * lax.rsqrt(jnp.mean(o * o, axis=-1, keepdims=True) + EPS) * norm_gain.astype(jnp.float32)
    o = o.reshape(B, S, H * V) * jax.nn.silu(g_out.astype(jnp.float32))
    return o.astype(dt)


def setup_inputs(seed: int = 0) -> dict:
    key = jax.random.key(seed)
    ks = jax.random.split(key, 14)
    nrm = lambda k, shape, fan_in: jax.random.normal(k, shape, jnp.float32) * (fan_in ** -0.5)
    gain = lambda k, shape: 1.0 + 0.02 * jax.random.normal(k, shape, jnp.float32)
    HV = HGRN_HEADS * HGRN_VAL_DIM
    return {
        "x": jax.random.normal(ks[0], (BATCH, SEQ, D_MODEL), jnp.float32),
        "ffn_norm": gain(ks[1], (DEPTH, 2, D_MODEL)),
        "ffn_w_in": nrm(ks[2], (DEPTH, 2, D_MODEL, 2 * D_FF), D_MODEL),
        "ffn_w_out": nrm(ks[3], (DEPTH, 2, D_FF, D_MODEL), D_FF),
        "mix_norm": gain(ks[4], (DEPTH, D_MODEL)),
        "w_in": nrm(ks[5], (DEPTH, D_MODEL, IN_WIDTH), D_MODEL),
        "b_gate": 0.01 * jax.random.normal(ks[6], (DEPTH, 2 * D_MODEL), jnp.float32),
        "hgrn_lb": 1.0 + 0.1 * jax.random.normal(ks[7], (DEPTH, HGRN_WIDTH), jnp.float32),
        "hgrn_norm": gain(ks[8], (DEPTH, HGRN_VAL_DIM)),
        "w_proj_attn": nrm(ks[9], (DEPTH, ATTN_OUT_WIDTH, D_MODEL), ATTN_OUT_WIDTH),
        "w_proj_hgrn": nrm(ks[10], (DEPTH, HV, D_MODEL), HV),
        "w_out": nrm(ks[11], (DEPTH, D_MODEL, D_MODEL), D_MODEL),
        "final_norm": gain(ks[12], (D_MODEL,)),
    }


def reference(x, ffn_norm, ffn_w_in, ffn_w_out, mix_norm, w_in, b_gate, hgrn_lb, hgrn_norm,
              w_proj_attn, w_proj_hgrn, w_out, final_norm):
    p = jax.nn.softmax(hgrn_lb.astype(jnp.float32), axis=0)
    lower_bounds = jnp.cumsum(p, axis=0) - p[0:1]
    split_idx = np.cumsum(SPLIT_SIZES)[:-1].tolist()
    for l in range(DEPTH):
        x = x + MACARON_WEIGHT * swiglu(rms_norm(x, ffn_norm[l, 0]), ffn_w_in[l, 0], ffn_w_out[l, 0])
        h = rms_norm(x, mix_norm[l])
        z = h @ w_in[l]
        q_a, k_a, v_a, q_h, f_h, i_h, g_h, gate_logits = jnp.split(z, split_idx, axis=-1)
        a = dilated_attention_branch(q_a, k_a, v_a) @ w_proj_attn[l]
        m = hgrn2_branch(q_h, f_h, i_h, g_h, lower_bounds[l], hgrn_norm[l]) @ w_proj_hgrn[l]
        gate_a, gate_m = jnp.split(jax.nn.sigmoid(gate_logits + b_gate[l]), 2, axis=-1)
        x = x + (gate_a * a + gate_m * m) @ w_out[l]
        x = x + MACARON_WEIGHT * swiglu(rms_norm(x, ffn_norm[l, 1]), ffn_w_in[l, 1], ffn_w_out[l, 1])
    return rms_norm(x, final_norm)
```

```python
import math
import numpy as np
import ml_dtypes
import concourse.bass as bass
import concourse.mybir as mybir
from concourse.bass_utils import run_bass_kernel_spmd

F32 = mybir.dt.float32
BF16 = mybir.dt.bfloat16
AF = mybir.ActivationFunctionType
ALU = mybir.AluOpType

D = 2048
T = 2048
DFF = 5504
NCH = D // 128
NJ = DFF // 128
DEPTH = 4
INW = 12800
EPS = 1e-6
N_CORES = 8
NSEQ = 1

OFF_QA, OFF_KA, OFF_VA = 0, 1536, 3072
OFF_QH, OFF_FH, OFF_IH, OFF_GH = 4608, 5632, 6656, 7680
OFF_GATE = 8704

CV_FFN = 0
CV_MIX = 128
CV_FIN = 192
CV_BG = 208
CV_LB = 336
CV_GN = 368
CV_MASK = 372
CV_RST = 628
NCV = 1140

SAME_SYNC = True
SEM_EPOCH = 20000


class Prog:
    ENGS = ("pe", "act", "dve", "pool", "sp")

    def __init__(self, nc):
        self.nc = nc
        self.ops = []
        self.cc_slots = set()

    def add(self, eng, fn, r=(), w=(), dma=None, free=False, cc=False):
        r = tuple(r)
        if not free:
            r = r + ("phase",)
        if cc:
            self.cc_slots.add(dma)
        self.ops.append((eng, fn, r, tuple(w), dma))

    def barrier(self):
        self.ops.append(("pool", None, (), ("phase",), None))

    def emit(self, dummy):
        nc = self.nc
        ops = self.ops
        n = len(ops)
        last_w = {}
        readers = {}
        deps = [None] * n
        for i, (eng, fn, r, w, dma) in enumerate(ops):
            d = set()
            for k in r:
                j = last_w.get(k)
                if j is not None:
                    d.add(j)
            for k in w:
                j = last_w.get(k)
                if j is not None:
                    d.add(j)
                rd = readers.get(k)
                if rd:
                    d.update(rd.values())
            d.discard(i)
            deps[i] = d
            src = (eng, dma) if dma else eng
            for k in r:
                readers.setdefault(k, {})[src] = i
            for k in w:
                last_w[k] = i
                readers[k] = {}
        need = [False] * n
        fdeps = [None] * n
        for i in range(n):
            eng = ops[i][0]
            lst = []
            for j in deps[i]:
                ej, _, _, _, dj = ops[j]
                if dj is None and ej == eng:
                    if eng == "pe" or eng == "sp" or not SAME_SYNC:
                        continue
                need[j] = True
                lst.append(j)
            lst.sort()
            fdeps[i] = lst
        cnt = {e: 0 for e in self.ENGS}
        slot_cnt = {}
        sig = [None] * n
        for i in range(n):
            eng, fn, r, w, dma = ops[i]
            if dma:
                slot_cnt[dma] = slot_cnt.get(dma, 0) + 1
                sig[i] = (("dma", dma), (1 if dma in self.cc_slots else 16) * slot_cnt[dma])
            elif need[i]:
                c = cnt[eng]
                cnt[eng] = c + 1
                sig[i] = (("eng", eng, c // SEM_EPOCH), c % SEM_EPOCH + 1)
        sems = {}
        for s in sig:
            if s is not None and s[0] not in sems:
                sems[s[0]] = nc.alloc_semaphore("s_" + "_".join(str(t) for t in s[0]))
        self.n_sems = len(sems)
        by_eng = {e: [] for e in self.ENGS}
        for i in range(n):
            by_eng[ops[i][0]].append(i)
        waited = {}
        self.n_wait = 0

        def run(engname, e):
            for i in by_eng[engname]:
                eng, fn, r, w, dma = ops[i]
                for j in fdeps[i]:
                    sk, val = sig[j]
                    key = (engname, sk)
                    if waited.get(key, 0) >= val:
                        continue
                    if sk[0] == "eng":
                        later = [kk for kk in waited if kk[0] == engname and kk[1][0] == "eng"
                                 and kk[1][1] == sk[1] and kk[1][2] > sk[2]]
                        if later:
                            continue
                    waited[key] = val
                    e.wait_ge(sems[sk], val)
                    self.n_wait += 1
                if fn is None:
                    ins = e.nop()
                else:
                    ins = fn(e)
                if sig[i] is not None:
                    if dma in self.cc_slots:
                        ins.then_inc(sems[sig[i][0]])
                    else:
                        ins.then_inc(sems[sig[i][0]], 16 if dma else 1)

        with nc.Block() as block:
            @block.tensor
            def _(e):
                run("pe", e)

            @block.scalar
            def _(e):
                run("act", e)

            @block.vector
            def _(e):
                run("dve", e)

            @block.gpsimd
            def _(e):
                run("pool", e)

            @block.sync
            def _(e):
                run("sp", e)


class Arena:
    def __init__(self, ap, nwords):
        self.ap = ap
        self.n = nwords
        self.off = 0
        self.peak = 0

    def reset(self):
        self.off = 0

    def _take(self, nw):
        a = self.ap[:, self.off:self.off + nw]
        self.off += nw
        self.peak = max(self.peak, self.off)
        assert self.off <= self.n, ("arena overflow", self.off, self.n)
        return a

    def f32(self, *shape):
        n = int(np.prod(shape))
        a = self._take(n)
        if len(shape) == 2:
            a = a.rearrange("p (a b) -> p a b", a=shape[0])
        return a

    def bf16(self, *shape):
        n = int(np.prod(shape))
        a = self._take((n + 1) // 2).bitcast(BF16)
        if len(shape) == 2:
            a = a.rearrange("p (a b) -> p a b", a=shape[0])
        return a


class RR:
    def __init__(self, items):
        self.items = list(items)
        self.i = 0

    def next(self):
        v = self.items[self.i % len(self.items)]
        self.i += 1
        return v


def build(n_layers=DEPTH, do=("ffn1", "mix", "ffn2"), mix_sub=("attn", "hgrn", "proj"), nseq=1):
    nc = bass.Bass("TRN2", target_bir_lowering=False)
    P = Prog(nc)

    x_all = nc.dram_tensor("x", [nseq, T, D], F32, kind="ExternalInput").ap()
    WSPEC = {
        "ffn_w_in": ((n_layers, 2), D, 2 * DFF, 32),
        "ffn_w_out": ((n_layers, 2), DFF, D, 172),
        "w_in": ((n_layers,), D, INW, 32),
        "w_pa": ((n_layers,), 512, D, 64),
        "w_pm": ((n_layers,), 1024, D, 128),
        "w_o": ((n_layers,), D, D, 128),
    }
    fwi_d = nc.dram_tensor("ffn_w_in", [n_layers, 2, D, 2 * DFF], F32, kind="ExternalInput").ap()
    fwo_d = nc.dram_tensor("ffn_w_out", [n_layers, 2, DFF, D], F32, kind="ExternalInput").ap()
    win_d = nc.dram_tensor("w_in", [n_layers, D, INW], F32, kind="ExternalInput").ap()
    wpa_d = nc.dram_tensor("w_pa", [n_layers, 512, D], F32, kind="ExternalInput").ap()
    wpm_d = nc.dram_tensor("w_pm", [n_layers, 1024, D], F32, kind="ExternalInput").ap()
    wo_d = nc.dram_tensor("w_o", [n_layers, D, D], F32, kind="ExternalInput").ap()

    def wkeys(nm, idx, r0=None, r1=None):
        lead, rtot, ccols, rp = WSPEC[nm]
        npc = rtot // (8 * rp)
        if r0 is None:
            ks = range(npc)
        else:
            ks = range(r0 // (8 * rp), (r1 - 1) // (8 * rp) + 1)
        return [("wf", nm, idx, k) for k in ks]

    cv_d = nc.dram_tensor("cvec", [128, NCV], F32, kind="ExternalInput").ap()
    id_d = nc.dram_tensor("ident", [128, 128], F32, kind="ExternalInput").ap()
    out_all = nc.dram_tensor("out", [nseq, T, D], F32, kind="ExternalOutput").ap()
    xT_all = nc.dram_tensor("xT_scr", [nseq, NCH, 128, T], F32, kind="Internal").ap()

    def sb(name, shape, dt):
        return nc.alloc_sbuf_tensor(name, shape, dt).ap()

    cv = sb("cv", [128, NCV], F32)
    ident = sb("ident_sb", [128, 128], F32)
    identb = sb("identb", [128, 128], BF16)
    ones32 = sb("ones32", [128, 128], F32)
    onesb = sb("onesb", [128, 128], BF16)
    maskb = sb("maskb", [128, 256], BF16)
    epst = sb("epst", [128, 1], F32)
    lbt = sb("lbt", [128, 32], F32)
    omlt = sb("omlt", [128, 32], F32)
    lbtmp = sb("lbtmp", [128, 48], F32)
    dummy = sb("dmy0", [128, 1], F32)
    WA_N = 4
    wa = [sb("wa%d" % i, [128, 16, 256], BF16) for i in range(WA_N)]
    ARENA_WORDS = 42400
    arena = Arena(sb("arena", [128, ARENA_WORDS], F32), ARENA_WORDS)
    ps = [nc.alloc_psum_tensor("ps%d" % i, [128, 512], F32).ap() for i in range(8)]

    wa_rr = RR(range(WA_N))

    P.add("sp", lambda e: e.dma_start(out=cv, in_=cv_d), w=["cv"], dma="cv", free=True)
    P.add("sp", lambda e: e.dma_start(out=ident, in_=id_d), w=["ident"], dma="ident", free=True)
    P.add("dve", lambda e: e.memset(ones32, 1.0), w=["ones32"], free=True)
    P.add("dve", lambda e: e.memset(onesb, 1.0), w=["onesb"], free=True)
    P.add("dve", lambda e: e.memset(epst, EPS), w=["epst"], free=True)
    P.add("dve", lambda e: e.tensor_copy(out=maskb, in_=cv[:, CV_MASK:CV_MASK + 256]),
          r=["cv"], w=["maskb"], free=True)
    P.add("dve", lambda e: e.tensor_copy(out=identb, in_=ident), r=["ident"], w=["identb"], free=True)
    ex = lbtmp[:, 0:32]
    ssum = lbtmp[:, 32:40]
    rs = lbtmp[:, 40:48]
    P.add("act", lambda e: e.activation(out=ex, in_=cv[:, CV_LB:CV_LB + 32], func=AF.Exp),
          r=["cv"], w=["lb_ex"], free=True)
    P.add("dve", lambda e: e.tensor_tensor(out=ssum, in0=ex[:, 0:8], in1=ex[:, 8:16], op=ALU.add),
          r=["lb_ex"], w=["lb_s"], free=True)
    P.add("dve", lambda e: e.tensor_tensor(out=ssum, in0=ssum, in1=ex[:, 16:24], op=ALU.add),
          r=["lb_ex", "lb_s"], w=["lb_s"], free=True)
    P.add("dve", lambda e: e.tensor_tensor(out=ssum, in0=ssum, in1=ex[:, 24:32], op=ALU.add),
          r=["lb_ex", "lb_s"], w=["lb_s"], free=True)
    P.add("dve", lambda e: e.reciprocal(out=rs, in_=ssum), r=["lb_s"], w=["lb_r"], free=True)
    P.add("dve", lambda e: e.memset(lbt[:, 0:8], 0.0), w=["lbt0"], free=True)
    for l in range(1, DEPTH):
        def f(e, l=l):
            return e.tensor_tensor(out=lbt[:, 8 * l:8 * l + 8], in0=ex[:, 8 * l:8 * l + 8], in1=rs, op=ALU.mult)
        P.add("dve", f, r=["lb_ex", "lb_r"], w=["lbt%d" % l], free=True)
    for l in range(2, DEPTH):
        def f(e, l=l):
            return e.tensor_tensor(out=lbt[:, 8 * l:8 * l + 8], in0=lbt[:, 8 * l:8 * l + 8],
                                   in1=lbt[:, 8 * l - 8:8 * l], op=ALU.add)
        P.add("dve", f, r=["lbt%d" % l, "lbt%d" % (l - 1)], w=["lbt%d" % l], free=True)
    P.add("dve", lambda e: e.tensor_scalar(out=omlt, in0=lbt, scalar1=-1.0, scalar2=1.0,
                                            op0=ALU.mult, op1=ALU.add),
          r=["lbt%d" % l for l in range(DEPTH)], w=["omlt"], free=True)
    LBK = ["lbt%d" % l for l in range(DEPTH)] + ["omlt"]

    def load_w(slot_ap, src_ap, key, slotname, rk=()):
        keys = key if isinstance(key, list) else [key]
        P.add("pool", lambda e: e.dma_start(out=slot_ap, in_=src_ap), r=list(rk), w=keys, dma=slotname, free=True)

    def wrows(w2d, c0, ncols):
        return w2d[:, c0:c0 + ncols].rearrange("(c p) n -> p c n", p=128)

    def mm_group(out_ps, pairs, r, w):
        def fn(e):
            ins = None
            n = len(pairs)
            for idx, (l, rh) in enumerate(pairs):
                ins = e.matmul(out_ps, lhsT=l, rhs=rh, start=(idx == 0), stop=(idx == n - 1))
            return ins
        P.add("pe", fn, r=r, w=w)

    def rms_rstd(x_tile, ntok, rstd, sq_bufs, tagx, ss_banks, nfeat, after_tt=None):
        for tt in range(ntok // 512):
            b = ss_banks.next()
            for c in range(NCH):
                s = sq_bufs.next()
                sqt, sqk = s

                def f1(e, c=c, tt=tt, sqt=sqt):
                    return e.activation(out=sqt, in_=x_tile[:, c, tt * 512:(tt + 1) * 512], func=AF.Square)
                P.add("act", f1, r=[(tagx, c)], w=[sqk])

                def f2(e, c=c, sqt=sqt, b=b):
                    return e.matmul(ps[b], lhsT=onesb, rhs=sqt, start=(c == 0), stop=(c == NCH - 1))
                P.add("pe", f2, r=[sqk, "onesb"] + ([("ps", b)] if c > 0 else []), w=[("ps", b)])

            def f3(e, tt=tt, b=b):
                return e.activation(out=rstd[:, tt * 512:(tt + 1) * 512], in_=ps[b], func=AF.Sqrt,
                                    bias=epst, scale=1.0 / nfeat)
            P.add("act", f3, r=[("ps", b), "epst"], w=[("rstd", tt)])

            def f4(e, tt=tt):
                return e.reciprocal(out=rstd[:, tt * 512:(tt + 1) * 512], in_=rstd[:, tt * 512:(tt + 1) * 512])
            P.add("dve", f4, r=[("rstd", tt)], w=[("rstd", tt)])
            if after_tt is not None:
                after_tt(tt)

    def pipeline(x_d, out_d, xT_d, seqi):
        P.barrier()
        arena.reset()
        xin = [arena.f32(D) for _ in range(3)]
        xst = [arena.f32(NCH, 128) for _ in range(2)]
        bank_rr = RR(range(8))

        def load_tile(t16):
            si = t16 % 3
            P.add("sp", lambda e, si=si, t16=t16: e.dma_start(out=xin[si], in_=x_d[t16 * 128:(t16 + 1) * 128, :]),
                  w=[("xin", si)], dma="xin%d" % si)
        load_tile(0)
        load_tile(1)
        for t16 in range(T // 128):
            s = t16 % 3
            s2 = t16 % 2
            for q in range(4):
                b = bank_rr.next()

                def ft(e, s=s, q=q, b=b):
                    ins = None
                    for j in range(4):
                        c = q * 4 + j
                        ins = e.transpose(out=ps[b][:, j * 128:(j + 1) * 128], in_=xin[s][:, c * 128:(c + 1) * 128],
                                          identity=ident)
                    return ins
                P.add("pe", ft, r=[("xin", s), "ident"], w=[("ps", b)])
                eng = "act" if q % 2 == 0 else "dve"

                def fc(e, s2=s2, q=q, b=b, eng=eng):
                    o = xst[s2][:, q * 4:(q + 1) * 4, :]
                    i = ps[b].rearrange("p (a b) -> p a b", a=4)
                    if eng == "act":
                        return e.activation(out=o, in_=i, func=AF.Copy)
                    return e.tensor_copy(out=o, in_=i)
                P.add(eng, fc, r=[("ps", b)], w=[("xst", s2, q)])
            P.add("sp", lambda e, s2=s2, t16=t16: e.dma_start(
                out=xT_d[:, :, t16 * 128:(t16 + 1) * 128].rearrange("c p t -> p c t"), in_=xst[s2]),
                r=[("xst", s2, q) for q in range(4)], w=[("xT", c, t16 // 4) for c in range(NCH)], dma="xst%d" % s2)
            if t16 + 2 < T // 128:
                load_tile(t16 + 2)

        def ffn_phase(l, i):
            P.barrier()
            arena.reset()
            x32 = arena.f32(NCH, 1024)
            hT = arena.bf16(NCH, 1024)
            gT = [arena.bf16(4, 1024) for _ in range(2)]
            wb = [arena.bf16(4, 2048) for _ in range(2)]
            sq = [arena.bf16(512) for _ in range(2)]
            rstd = arena.f32(1024)
            sg = [arena.f32(512) for _ in range(2)]
            w_in = fwi_d[l, i]
            w_out = fwo_d[l, i]
            gcol = CV_FFN + (l * 2 + i) * NCH
            gu_rr = RR([(0, 1), (2, 3)])
            y_rr = RR([4, 5])
            ss_rr = RR([6, 7])
            sq_rr = RR([(sq[0], "sq0"), (sq[1], "sq1")])
            sg_rr = RR([0, 1])
            g_rr = RR([0, 1])
            wb_rr = RR([0, 1])
            groups = []
            j = 0
            while j < NJ:
                groups.append(list(range(j, min(j + 4, NJ))))
                j += 4
            for half in range(2):
                T0 = half * 1024
                for c in range(NCH):
                    P.add("sp", lambda e, c=c, T0=T0: e.dma_start(out=x32[:, c, :], in_=xT_d[c, :, T0:T0 + 1024]),
                          r=[("xT", c, 2 * half), ("xT", c, 2 * half + 1)], w=[("x32", c)], dma="x32_%d" % c)
                def emit_h(tt):
                    for c in range(NCH):
                        def fh(e, c=c, tt=tt):
                            return e.scalar_tensor_tensor(
                                out=hT[:, c, tt * 512:(tt + 1) * 512], in0=x32[:, c, tt * 512:(tt + 1) * 512],
                                scalar=cv[:, gcol + c:gcol + c + 1], in1=rstd[:, tt * 512:(tt + 1) * 512],
                                op0=ALU.mult, op1=ALU.mult)
                        P.add("dve", fh, r=[("x32", c), ("rstd", tt), "cv"], w=[("hT", c, tt)])
                rms_rstd(x32, 1024, rstd, sq_rr, "x32", ss_rr, D, after_tt=emit_h)
                for gi, grp in enumerate(groups):
                    gs = g_rr.next()
                    slots_ = {}
                    for jj, j in enumerate(grp):
                        s = wa_rr.next()
                        slots_[jj] = s
                        load_w(wa[s][:, :, 0:128], wrows(w_in, j * 128, 128), ("wa", s, 0), "wa%d_0" % s, wkeys("ffn_w_in", (l, i)))
                        load_w(wa[s][:, :, 128:256], wrows(w_in, DFF + j * 128, 128), ("wa", s, 1), "wa%d_1" % s, wkeys("ffn_w_in", (l, i)))
                    if gi == 0:
                        order = [(jj, tt) for tt in range(2) for jj in range(len(grp))]
                    else:
                        order = [(jj, tt) for jj in range(len(grp)) for tt in range(2)]
                    for jj, tt in order:
                        s = slots_[jj]
                        bg, bu = gu_rr.next()
                        hk = [("hT", c, tt) for c in range(NCH)]
                        mm_group(ps[bg], [(wa[s][:, c, 0:128], hT[:, c, tt * 512:(tt + 1) * 512]) for c in range(NCH)],
                                 r=[("wa", s, 0)] + hk, w=[("ps", bg)])
                        mm_group(ps[bu], [(wa[s][:, c, 128:256], hT[:, c, tt * 512:(tt + 1) * 512]) for c in range(NCH)],
                                 r=[("wa", s, 1)] + hk, w=[("ps", bu)])
                        si = sg_rr.next()
                        P.add("act", lambda e, si=si, bg=bg: e.activation(out=sg[si], in_=ps[bg], func=AF.Silu),
                              r=[("ps", bg)], w=[("sg", si)])
                        P.add("dve", lambda e, si=si, bu=bu, gs=gs, jj=jj, tt=tt: e.tensor_tensor(
                            out=gT[gs][:, jj, tt * 512:(tt + 1) * 512], in0=sg[si], in1=ps[bu], op=ALU.mult),
                            r=[("sg", si), ("ps", bu)], w=[("gT", gs, jj, tt)])
                    ws = wb_rr.next()
                    G = len(grp)
                    j0 = grp[0]
                    P.add("pool", lambda e, ws=ws, G=G, j0=j0: e.dma_start(
                        out=wb[ws][:, 0:G, :], in_=w_out[j0 * 128:(j0 + G) * 128, :].rearrange("(g p) n -> p g n", p=128)),
                        r=wkeys("ffn_w_out", (l, i), j0 * 128, (j0 + G) * 128), w=[("wb", ws)], dma="wb%d" % ws)
                    for m in range(NCH):
                        for tt in range(2):
                            by = y_rr.next()
                            mm_group(ps[by], [(wb[ws][:, jj, m * 128:(m + 1) * 128], gT[gs][:, jj, tt * 512:(tt + 1) * 512])
                                              for jj in range(G)],
                                     r=[("wb", ws)] + [("gT", gs, jj, tt) for jj in range(G)], w=[("ps", by)])
                            P.add("dve", lambda e, by=by, m=m, tt=tt: e.scalar_tensor_tensor(
                                out=x32[:, m, tt * 512:(tt + 1) * 512], in0=ps[by], scalar=0.5,
                                in1=x32[:, m, tt * 512:(tt + 1) * 512], op0=ALU.mult, op1=ALU.add),
                                r=[("ps", by), ("x32", m)], w=[("x32", m)])
                for c in range(NCH):
                    P.add("sp", lambda e, c=c, T0=T0: e.dma_start(out=xT_d[c, :, T0:T0 + 1024], in_=x32[:, c, :]),
                          r=[("x32", c)], w=[("xT", c, 2 * half), ("xT", c, 2 * half + 1)], dma="x32s_%d" % c)

        def mixer_fixed():
            arena.reset()
            hT = arena.bf16(NCH, T)
            aT = arena.bf16(4, T)
            mT = arena.bf16(8, T)
            return hT, aT, mT

        def mixer_norm(l):
            P.barrier()
            hT, aT, mT = mixer_fixed()
            TW = 256
            xts = [arena.f32(NCH, TW) for _ in range(2)]
            sq = [arena.bf16(TW) for _ in range(2)]
            rstds = [arena.f32(TW) for _ in range(2)]
            gcol = CV_MIX + l * NCH
            sq_rr = RR([0, 1])
            for ti in range(T // TW):
                bi = ti % 2
                xt = xts[bi]
                rstd = rstds[bi]
                tq = (ti * TW) // 512
                t0 = ti * TW
                b = 6 + bi
                for q4 in range(4):
                    P.add("sp", lambda e, t0=t0, xt=xt, q4=q4: e.dma_start(
                        out=xt[:, q4 * 4:(q4 + 1) * 4, :],
                        in_=xT_d[q4 * 4:(q4 + 1) * 4, :, t0:t0 + TW].rearrange("c p t -> p c t")),
                        r=[("xT", c, tq) for c in range(q4 * 4, q4 * 4 + 4)],
                        w=[("xtn", bi, c) for c in range(q4 * 4, q4 * 4 + 4)], dma="xtn%d_%d" % (bi, q4))
                for c in range(NCH):
                    si = sq_rr.next()
                    P.add("act", lambda e, c=c, si=si, xt=xt: e.activation(out=sq[si], in_=xt[:, c, :], func=AF.Square),
                          r=[("xtn", bi, c)], w=[("sqn", si)])
                    P.add("pe", lambda e, c=c, si=si, b=b: e.matmul(ps[b][:, 0:TW], lhsT=onesb, rhs=sq[si],
                                                                     start=(c == 0), stop=(c == NCH - 1)),
                          r=[("sqn", si), "onesb"] + ([("ps", b)] if c > 0 else []), w=[("ps", b)])
                P.add("act", lambda e, b=b, rstd=rstd: e.activation(out=rstd, in_=ps[b][:, 0:TW], func=AF.Sqrt,
                                                                     bias=epst, scale=1.0 / D),
                      r=[("ps", b), "epst"], w=[("rstdn", bi)])
                P.add("dve", lambda e, rstd=rstd: e.reciprocal(out=rstd, in_=rstd), r=[("rstdn", bi)], w=[("rstdn", bi)])
                for c in range(NCH):
                    P.add("dve", lambda e, c=c, t0=t0, xt=xt, rstd=rstd: e.scalar_tensor_tensor(
                        out=hT[:, c, t0:t0 + TW], in0=xt[:, c, :], scalar=cv[:, gcol + c:gcol + c + 1], in1=rstd,
                        op0=ALU.mult, op1=ALU.mult),
                        r=[("xtn", bi, c), ("rstdn", bi), "cv"], w=[("hT", c, tq)])

        def proj_cols(hT, w2d, col0, slot, half, banks, tts=range(4)):
            for tt in tts:
                b = banks[tt]
                mm_group(ps[b], [(wa[slot][:, c, half * 128:(half + 1) * 128], hT[:, c, tt * 512:(tt + 1) * 512])
                                 for c in range(NCH)],
                         r=[("wa", slot, half)] + [("hT", c, tt) for c in range(NCH)], w=[("ps", b)])

        def mixer_attn(l):
            P.barrier()
            hT, aT, mT = mixer_fixed()
            num = arena.f32(T)
            den = arena.f32(T)
            qkv = [[arena.bf16(T) for _ in range(3)] for _ in range(2)]
            vtok = [arena.bf16(16, 128) for _ in range(2)]
            E = [arena.bf16(256) for _ in range(8)]
            w2d = win_d[l]
            set_rr = RR([0, 1])
            e_rr = RR(range(8))
            scale = 1.0 / math.sqrt(128.0)
            evac_rr = RR(["act", "dve"])
            for h in range(4):
                for g, r_ in enumerate((1, 4, 16)):
                    L = T // r_
                    nbs = L // 128
                    st = set_rr.next()
                    qT, kT, vT = qkv[st]
                    s0 = wa_rr.next()
                    load_w(wa[s0][:, :, 0:128], wrows(w2d, OFF_QA + g * 512 + h * 128, 128), ("wa", s0, 0), "wa%d_0" % s0, wkeys("w_in", (l,)))
                    load_w(wa[s0][:, :, 128:256], wrows(w2d, OFF_KA + g * 512 + h * 128, 128), ("wa", s0, 1), "wa%d_1" % s0, wkeys("w_in", (l,)))
                    s1 = wa_rr.next()
                    load_w(wa[s1][:, :, 0:128], wrows(w2d, OFF_VA + g * 512 + h * 128, 128), ("wa", s1, 0), "wa%d_0" % s1, wkeys("w_in", (l,)))
                    for (slot, hf, dst, nm) in ((s0, 0, qT, "q"), (s0, 1, kT, "k"), (s1, 0, vT, "v")):
                        banks = [0, 1, 2, 3] if nm != "k" else [4, 5, 6, 7]
                        proj_cols(hT, w2d, 0, slot, hf, banks)
                        for tt in range(4):
                            eng = evac_rr.next()

                            def fe(e, dst=dst, tt=tt, b=banks[tt], eng=eng):
                                if eng == "act":
                                    return e.activation(out=dst[:, tt * 512:(tt + 1) * 512], in_=ps[b], func=AF.Copy)
                                return e.tensor_copy(out=dst[:, tt * 512:(tt + 1) * 512], in_=ps[b])
                            P.add(eng, fe, r=[("ps", banks[tt])], w=[(nm, st, tt)])
                    qv = qT.rearrange("p (m r) -> p r m", r=r_)
                    kv = kT.rearrange("p (m r) -> p r m", r=r_)
                    vv = vT.rearrange("p (m r) -> p r m", r=r_)
                    nv = num.rearrange("p (m r) -> p r m", r=r_)
                    dv = den.rearrange("p (m r) -> p r m", r=r_)
                    allq = [("q", st, tt) for tt in range(4)]
                    allk = [("k", st, tt) for tt in range(4)]
                    allv = [("v", st, tt) for tt in range(4)]
                    for half8 in range(2):
                        b = 0 + half8
                        pb = ps[b].bitcast(BF16)

                        def ftr(e, half8=half8, pb=pb, vv=vv, nbs=nbs):
                            ins = None
                            for k in range(8):
                                B = half8 * 8 + k
                                c_, n_ = B // nbs, B % nbs
                                ins = e.transpose(out=pb[:, k * 128:(k + 1) * 128],
                                                  in_=vv[:, c_, n_ * 128:(n_ + 1) * 128], identity=identb)
                            return ins
                        P.add("pe", ftr, r=allv + ["identb"], w=[("ps", b)])
                        P.add("dve", lambda e, half8=half8, pb=pb, st=st: e.tensor_copy(
                            out=vtok[st][:, half8 * 8:(half8 + 1) * 8, :], in_=pb.rearrange("p (a b) -> p a b", a=8)),
                            r=[("ps", b)], w=[("vtok", st, half8)])
                    Eof = {}
                    for B in range(16):
                        c_, n_ = B // nbs, B % nbs
                        wid = 256 if n_ + 1 < nbs else 128
                        b = 2 + (B % 2)
                        ei = e_rr.next()
                        Eof[B] = ei

                        def fs(e, c_=c_, n_=n_, wid=wid, b=b, kv=kv, qv=qv):
                            return e.matmul(ps[b][:, 0:wid], lhsT=kv[:, c_, n_ * 128:(n_ + 1) * 128],
                                            rhs=qv[:, c_, n_ * 128:n_ * 128 + wid], start=True, stop=True)
                        P.add("pe", fs, r=allq + allk, w=[("ps", b)])
                        P.add("act", lambda e, b=b, wid=wid, ei=ei: e.activation(
                            out=E[ei][:, 0:wid], in_=ps[b][:, 0:wid], func=AF.Exp, scale=scale),
                            r=[("ps", b)], w=[("E", ei)])
                        P.add("dve", lambda e, wid=wid, ei=ei: e.tensor_tensor(
                            out=E[ei][:, 0:wid], in0=E[ei][:, 0:wid], in1=maskb[:, 0:wid], op=ALU.mult),
                            r=[("E", ei), "maskb"], w=[("E", ei)])
                        if B % 4 == 3:
                            B0 = B - 3
                            bo, bd = 4 + ((B // 4) % 2) * 2, 5 + ((B // 4) % 2) * 2

                            def fo(e, B0=B0, bo=bo, bd=bd, nbs=nbs, st=st, Eof=dict(Eof)):
                                ins = None
                                for lhs_kind, bank in (("v", bo), ("1", bd)):
                                    for k in range(4):
                                        Bq = B0 + k
                                        nq = Bq % nbs
                                        o = ps[bank][:, k * 128:(k + 1) * 128]
                                        has_prev = nq > 0
                                        if has_prev:
                                            lp = vtok[st][:, Bq - 1, :] if lhs_kind == "v" else onesb
                                            ins = e.matmul(o, lhsT=lp, rhs=E[Eof[Bq - 1]][:, 128:256], start=True, stop=False)
                                        lc = vtok[st][:, Bq, :] if lhs_kind == "v" else onesb
                                        ins = e.matmul(o, lhsT=lc, rhs=E[Eof[Bq]][:, 0:128], start=(not has_prev), stop=True)
                                return ins
                            need_e = [("E", Eof[bb]) for bb in range(max(B0 - 1, 0), B + 1)]
                            P.add("pe", fo, r=need_e + [("vtok", st, 0), ("vtok", st, 1), "onesb"],
                                  w=[("ps", bo), ("ps", bd)])
                            if r_ == 1:
                                no = nv[:, 0, B0 * 128:(B0 + 4) * 128]
                                do_ = dv[:, 0, B0 * 128:(B0 + 4) * 128]
                                pso, psd = ps[bo], ps[bd]
                            elif r_ == 4:
                                no = nv[:, B0 // 4, :]
                                do_ = dv[:, B0 // 4, :]
                                pso, psd = ps[bo], ps[bd]
                            else:
                                no = nv[:, B0:B0 + 4, :]
                                do_ = dv[:, B0:B0 + 4, :]
                                pso = ps[bo].rearrange("p (a b) -> p a b", a=4)
                                psd = ps[bd].rearrange("p (a b) -> p a b", a=4)
                            if g == 0:
                                P.add("act", lambda e, no=no, pso=pso: e.activation(out=no, in_=pso, func=AF.Copy),
                                      r=[("ps", bo)], w=[("num", B0 // 4)])
                                P.add("dve", lambda e, do_=do_, psd=psd: e.tensor_copy(out=do_, in_=psd),
                                      r=[("ps", bd)], w=[("den", B0 // 4)])
                            else:
                                allnum = [("num", k) for k in range(4)]
                                allden = [("den", k) for k in range(4)]
                                P.add("dve", lambda e, no=no, pso=pso: e.tensor_tensor(out=no, in0=pso, in1=no, op=ALU.add),
                                      r=[("ps", bo)] + allnum, w=allnum)
                                P.add("dve", lambda e, do_=do_, psd=psd: e.tensor_tensor(out=do_, in0=psd, in1=do_, op=ALU.add),
                                      r=[("ps", bd)] + allden, w=allden)
                allnum = [("num", k) for k in range(4)]
                allden = [("den", k) for k in range(4)]
                P.add("dve", lambda e: e.reciprocal(out=den, in_=den), r=allden, w=allden)
                P.add("dve", lambda e, h=h: e.tensor_tensor(out=aT[:, h, :], in0=num, in1=den, op=ALU.mult),
                      r=allnum + allden, w=[("aT", h)])

        def mixer_hgrn(l):
            P.barrier()
            hT, aT, mT = mixer_fixed()
            w2d = win_d[l]
            tf = arena.f32(512)
            tlf = arena.f32(512)
            tb = arena.f32(512)
            td1 = arena.f32(512)
            td2 = arena.f32(512)
            te1 = arena.f32(512)
            teb = arena.f32(512)
            tqs = arena.f32(512)
            tgss = [arena.f32(512) for _ in range(2)]
            pending = [None]
            tosb = arena.f32(512)
            tsq = arena.bf16(512)
            trr = arena.f32(512)
            qt_ = arena.bf16(512)
            qh_ = arena.bf16(512)
            kt_ = arena.bf16(512)
            kh_ = arena.bf16(512)
            vT_ = arena.bf16(512)
            tok = arena.bf16(8, 128)
            A4 = arena.bf16(4, 128)
            S = arena.f32(128)
            Sbs = [arena.bf16(4, 128) for _ in range(2)]
            Sbz = arena.bf16(128)
            ebl = arena.f32(4)
            P.add("dve", lambda e: e.memset(A4, 0.0), w=["A4"])
            P.add("dve", lambda e: e.memset(Sbz, 0.0), w=["Sbz"])
            stepi = [0]
            lcol = l * 8
            a_rr = RR([0, 1])
            hslots = {}

            def emit_proj(hh, tt_, pieces):
                if hh not in hslots:
                    s0 = wa_rr.next()
                    load_w(wa[s0][:, :, 0:128], wrows(w2d, OFF_QH + hh * 128, 128), ("wa", s0, 0), "wa%d_0" % s0)
                    load_w(wa[s0][:, :, 128:256], wrows(w2d, OFF_FH + hh * 128, 128), ("wa", s0, 1), "wa%d_1" % s0)
                    s1 = wa_rr.next()
                    load_w(wa[s1][:, :, 0:128], wrows(w2d, OFF_IH + hh * 128, 128), ("wa", s1, 0), "wa%d_0" % s1)
                    load_w(wa[s1][:, :, 128:256], wrows(w2d, OFF_GH + hh * 128, 128), ("wa", s1, 1), "wa%d_1" % s1)
                    hslots[hh] = (s0, s1)
                s0, s1 = hslots[hh]
                for p_ in pieces:
                    slot, half = ((s0, 0), (s0, 1), (s1, 0), (s1, 1))[p_]
                    proj_cols(hT, w2d, 0, slot, half, {tt_: p_}, tts=[tt_])

            for h in range(8):
                P.add("dve", lambda e: e.memset(S, 0.0), w=["S"])
                lb_ap = lbt[:, lcol + h:lcol + h + 1]
                oml_ap = omlt[:, lcol + h:lcol + h + 1]
                for tt in range(4):
                    if (h, tt) == (0, 0):
                        emit_proj(0, 0, [0, 1, 2, 3])
                    nxt = (h, tt + 1) if tt < 3 else ((h + 1, 0) if h < 7 else None)
                    b3 = lambda a: a.rearrange("p (a b) -> p a b", a=4)
                    kpar = (h * 4 + tt) % 2
                    tgs = tgss[kpar]
                    P.add("act", lambda e: e.activation(out=tf, in_=ps[1], func=AF.Sigmoid), r=[("ps", 1)], w=["tf"])
                    P.add("act", lambda e: e.activation(out=tqs, in_=ps[0], func=AF.Silu), r=[("ps", 0)], w=["tqs"])
                    P.add("act", lambda e, tgs=tgs: e.activation(out=tgs, in_=ps[3], func=AF.Silu), r=[("ps", 3)], w=[("tgs", kpar)])
                    P.add("act", lambda e: e.activation(out=vT_, in_=ps[2], func=AF.Copy), r=[("ps", 2)], w=["vT"])
                    if nxt is not None:
                        emit_proj(nxt[0], nxt[1], [0, 1])
                    P.add("dve", lambda e, oml_ap=oml_ap, lb_ap=lb_ap: e.tensor_scalar(
                        out=tf, in0=tf, scalar1=oml_ap, scalar2=lb_ap, op0=ALU.mult, op1=ALU.add),
                        r=["tf"] + LBK, w=["tf"])
                    P.add("act", lambda e: e.activation(out=tlf, in_=tf, func=AF.Ln), r=["tf"], w=["tlf"])
                    P.add("dve", lambda e: e.tensor_scalar(out=tf, in0=tf, scalar1=-1.0, scalar2=1.0,
                                                            op0=ALU.mult, op1=ALU.add), r=["tf", "tlf"], w=["tf"])
                    P.add("dve", lambda e: e.tensor_tensor_scan(out=tb, data0=cv[:, CV_RST:CV_RST + 512], data1=tlf,
                                                                 initial=0.0, op0=ALU.mult, op1=ALU.add),
                          r=["tlf", "cv"], w=["tb"])
                    P.add("dve", lambda e: e.tensor_tensor(out=b3(td1), in0=b3(tb),
                                                            in1=b3(tb)[:, :, 63:64].broadcast_to([128, 4, 128]),
                                                            op=ALU.subtract), r=["tb"], w=["td1"])
                    P.add("dve", lambda e: e.tensor_tensor(out=b3(td2), in0=b3(tb)[:, :, 127:128].broadcast_to([128, 4, 128]),
                                                            in1=b3(tb), op=ALU.subtract), r=["tb"], w=["td2"])
                    P.add("act", lambda e: e.activation(out=te1, in_=td1, func=AF.Exp), r=["td1"], w=["te1"])
                    P.add("act", lambda e: e.activation(out=td1, in_=td1, func=AF.Exp, scale=-1.0), r=["td1", "te1"], w=["td1"])
                    P.add("act", lambda e: e.activation(out=td2, in_=td2, func=AF.Exp), r=["td2"], w=["td2"])
                    P.add("act", lambda e: e.activation(out=teb, in_=tb, func=AF.Exp), r=["tb"], w=["teb"])
                    P.add("act", lambda e: e.activation(out=ebl, in_=b3(tb)[:, :, 127], func=AF.Exp), r=["tb"], w=["ebl"])
                    if pending[0] is not None:
                        pending[0][0]()
                    P.add("dve", lambda e: e.tensor_tensor(out=qt_, in0=tqs, in1=te1, op=ALU.mult), r=["tqs", "te1"], w=["qt"])
                    P.add("dve", lambda e: e.tensor_tensor(out=qh_, in0=tqs, in1=teb, op=ALU.mult), r=["tqs", "teb"], w=["qh"])
                    P.add("dve", lambda e: e.tensor_tensor(out=kt_, in0=tf, in1=td1, op=ALU.mult), r=["tf", "td1"], w=["kt"])
                    P.add("dve", lambda e: e.tensor_tensor(out=kh_, in0=tf, in1=td2, op=ALU.mult), r=["tf", "td2"], w=["kh"])
                    pb = ps[4].bitcast(BF16)

                    def ftr(e, pb=pb):
                        ins = None
                        for k in range(4):
                            ins = e.transpose(out=pb[:, k * 128:(k + 1) * 128], in_=vT_[:, k * 128:(k + 1) * 128], identity=identb)
                        for k in range(4):
                            ins = e.transpose(out=pb[:, (4 + k) * 128:(5 + k) * 128], in_=kh_[:, k * 128:(k + 1) * 128],
                                              identity=identb)
                        return ins
                    P.add("pe", ftr, r=["vT", "kh", "identb"], w=[("ps", 4)])
                    if nxt is not None:
                        emit_proj(nxt[0], nxt[1], [2])
                    P.add("dve", lambda e, pb=pb: e.tensor_copy(out=tok, in_=pb.rearrange("p (a b) -> p a b", a=8)),
                          r=[("ps", 4)], w=["tok"])
                    st = stepi[0] % 2
                    stepi[0] += 1
                    Sb4 = Sbs[st]
                    Sprev = Sbs[1 - st]

                    def fa(e):
                        ins = None
                        for ch in range(4):
                            e.matmul(ps[5][:, ch * 128 + 64:(ch + 1) * 128], lhsT=kt_[:, ch * 128:(ch + 1) * 128],
                                     rhs=qt_[:, ch * 128 + 64:(ch + 1) * 128], start=True, stop=True)
                            ins = e.matmul(ps[5][0:64, ch * 128:ch * 128 + 64], lhsT=kt_[:, ch * 128:ch * 128 + 64],
                                           rhs=qt_[:, ch * 128:ch * 128 + 64], start=True, stop=True)
                        return ins
                    P.add("pe", fa, r=["kt", "qt"], w=[("ps", 5)])
                    p5 = ps[5].rearrange("p (a b) -> p a b", a=4)
                    P.add("dve", lambda e, p5=p5: e.tensor_tensor(
                        out=A4[:, :, 64:128], in0=p5[:, :, 64:128],
                        in1=maskb[:, 64:128].unsqueeze(1).broadcast_to([128, 4, 64]), op=ALU.mult),
                        r=[("ps", 5), "maskb"], w=["A4"])
                    P.add("dve", lambda e, p5=p5: e.tensor_tensor(
                        out=A4[0:64, :, 0:64], in0=p5[0:64, :, 0:64],
                        in1=maskb[0:64, 0:64].unsqueeze(1).broadcast_to([64, 4, 64]), op=ALU.mult),
                        r=[("ps", 5), "maskb", "A4"], w=["A4"])

                    def fsn(e):
                        ins = None
                        for ch in range(4):
                            ins = e.matmul(ps[7][:, ch * 128:(ch + 1) * 128], lhsT=tok[:, 4 + ch, :], rhs=tok[:, ch, :],
                                           start=True, stop=True)
                        return ins
                    P.add("pe", fsn, r=["tok"], w=[("ps", 7)])
                    if nxt is not None:
                        emit_proj(nxt[0], nxt[1], [3])
                    for ch in range(4):
                        P.add("dve", lambda e, ch=ch: e.scalar_tensor_tensor(
                            out=S, in0=S, scalar=ebl[:, ch:ch + 1], in1=ps[7][:, ch * 128:(ch + 1) * 128],
                            op0=ALU.mult, op1=ALU.add), r=["S", "ebl", ("ps", 7)], w=["S"])
                        P.add("act", lambda e, ch=ch, Sb4=Sb4: e.activation(out=Sb4[:, ch, :], in_=S, func=AF.Copy),
                              r=["S"], w=[("Sb", st, ch)])
                    if pending[0] is not None:
                        pending[0][1]()
                        pending[0] = None
                    for ch in range(4):
                        if ch == 0:
                            lhs_s = Sbz if tt == 0 else Sprev[:, 3, :]
                            rk = ["Sbz"] if tt == 0 else [("Sb", 1 - st, 3)]
                        else:
                            lhs_s = Sb4[:, ch - 1, :]
                            rk = [("Sb", st, ch - 1)]

                        def fo(e, ch=ch, lhs_s=lhs_s):
                            o = ps[6][:, ch * 128:(ch + 1) * 128]
                            e.matmul(o, lhsT=lhs_s, rhs=qh_[:, ch * 128:(ch + 1) * 128], start=True, stop=False)
                            return e.matmul(o, lhsT=tok[:, ch, :], rhs=A4[:, ch, :], start=False, stop=True)
                        P.add("pe", fo, r=rk + ["qh", "tok", "A4"], w=[("ps", 6)])
                    def tail_act():
                        P.add("act", lambda e: e.activation(out=tsq, in_=ps[6], func=AF.Square), r=[("ps", 6)], w=["tsq"])
                        P.add("pe", lambda e: e.matmul(ps[4], lhsT=onesb, rhs=tsq, start=True, stop=True),
                              r=["tsq", "onesb"], w=[("ps", 4)])
                        P.add("act", lambda e: e.activation(out=trr, in_=ps[4], func=AF.Ln, bias=epst, scale=1.0 / 128.0),
                              r=[("ps", 4), "epst"], w=["trr"])
                        P.add("act", lambda e: e.activation(out=trr, in_=trr, func=AF.Exp, scale=-0.5), r=["trr"], w=["trr"])

                    def tail_dve(h=h, tt=tt, tgs=tgs, kpar=kpar):
                        P.add("dve", lambda e: e.scalar_tensor_tensor(
                            out=tosb, in0=ps[6], scalar=cv[:, CV_GN + l:CV_GN + l + 1], in1=trr, op0=ALU.mult, op1=ALU.mult),
                            r=[("ps", 6), "trr", "cv"], w=["tosb"])
                        P.add("dve", lambda e: e.tensor_tensor(
                            out=mT[:, h, tt * 512:(tt + 1) * 512], in0=tosb, in1=tgs, op=ALU.mult),
                            r=["tosb", ("tgs", kpar)], w=[("mT", h, tt)])
                    pending[0] = (tail_act, tail_dve)
            if pending[0] is not None:
                pending[0][0]()
                pending[0][1]()
                pending[0] = None

        def mixer_proj(l):
            P.barrier()
            hT, aT, mT = mixer_fixed()
            if "attn" not in mix_sub:
                P.add("dve", lambda e: e.memset(aT, 0.0), w=[("aT", hh) for hh in range(4)])
            if "hgrn" not in mix_sub:
                P.add("dve", lambda e: e.memset(mT, 0.0), w=[("mT", hh, tq) for hh in range(8) for tq in range(4)])
            uT = arena.bf16(NCH, 1024)
            wc = [arena.bf16(12, 128) for _ in range(2)]
            ga = [arena.f32(512) for _ in range(2)]
            gm = [arena.f32(512) for _ in range(2)]
            xt = [arena.f32(512) for _ in range(3)]
            w2d = win_d[l]
            wc_rr = RR([0, 1])
            s_rr = RR([0, 1])
            x_rr = RR([0, 1, 2])
            y_rr = RR([0, 1, 2, 3, 4, 5, 6, 7])
            bset_rr = RR([(0, 1, 2, 3), (4, 5, 6, 7)])
            bgc = CV_BG + l * 32
            for th in range(2):
                for m in range(NCH):
                    wci = wc_rr.next()
                    P.add("pool", lambda e, wci=wci, m=m: e.dma_start(
                        out=wc[wci][:, 0:4, :], in_=wpa_d[l][:, m * 128:(m + 1) * 128].rearrange("(c p) n -> p c n", p=128)),
                        w=[("wc", wci, 0)], dma="wc%d_0" % wci)
                    P.add("pool", lambda e, wci=wci, m=m: e.dma_start(
                        out=wc[wci][:, 4:12, :], in_=wpm_d[l][:, m * 128:(m + 1) * 128].rearrange("(c p) n -> p c n", p=128)),
                        w=[("wc", wci, 1)], dma="wc%d_1" % wci)
                    s = wa_rr.next()
                    load_w(wa[s][:, :, 0:128], wrows(w2d, OFF_GATE + m * 128, 128), ("wa", s, 0), "wa%d_0" % s)
                    load_w(wa[s][:, :, 128:256], wrows(w2d, OFF_GATE + D + m * 128, 128), ("wa", s, 1), "wa%d_1" % s)
                    for t2 in range(2):
                        tq = th * 2 + t2
                        tsl = slice(tq * 512, (tq + 1) * 512)
                        usl = slice(t2 * 512, (t2 + 1) * 512)
                        hk = [("hT", c, tq) for c in range(NCH)]
                        B0, B1, B2, B3 = bset_rr.next()
                        mm_group(ps[B0], [(wa[s][:, c, 0:128], hT[:, c, tsl]) for c in range(NCH)],
                                 r=[("wa", s, 0)] + hk, w=[("ps", B0)])
                        mm_group(ps[B1], [(wa[s][:, c, 128:256], hT[:, c, tsl]) for c in range(NCH)],
                                 r=[("wa", s, 1)] + hk, w=[("ps", B1)])
                        mm_group(ps[B2], [(wc[wci][:, hh, :], aT[:, hh, tsl]) for hh in range(4)],
                                 r=[("wc", wci, 0)] + [("aT", hh) for hh in range(4)], w=[("ps", B2)])
                        mm_group(ps[B3], [(wc[wci][:, 4 + hh, :], mT[:, hh, tsl]) for hh in range(8)],
                                 r=[("wc", wci, 1)] + [("mT", hh, tq) for hh in range(8)], w=[("ps", B3)])
                        si = s_rr.next()
                        P.add("act", lambda e, si=si, m=m, B0=B0: e.activation(out=ga[si], in_=ps[B0], func=AF.Sigmoid,
                                                                       bias=cv[:, bgc + m:bgc + m + 1]),
                              r=[("ps", B0), "cv"], w=[("ga", si)])
                        P.add("act", lambda e, si=si, m=m, B1=B1: e.activation(out=gm[si], in_=ps[B1], func=AF.Sigmoid,
                                                                       bias=cv[:, bgc + 16 + m:bgc + 16 + m + 1]),
                              r=[("ps", B1), "cv"], w=[("gm", si)])
                        P.add("dve", lambda e, si=si, B2=B2: e.tensor_tensor(out=ga[si], in0=ga[si], in1=ps[B2], op=ALU.mult),
                              r=[("ga", si), ("ps", B2)], w=[("ga", si)])
                        P.add("dve", lambda e, si=si, B3=B3: e.tensor_tensor(out=gm[si], in0=gm[si], in1=ps[B3], op=ALU.mult),
                              r=[("gm", si), ("ps", B3)], w=[("gm", si)])
                        P.add("dve", lambda e, si=si, m=m, usl=usl: e.tensor_tensor(
                            out=uT[:, m, usl], in0=ga[si], in1=gm[si], op=ALU.add),
                            r=[("ga", si), ("gm", si)], w=[("uT", m, t2)])
                for m2 in range(NCH // 2):
                    s = wa_rr.next()
                    load_w(wa[s][:, :, 0:256], wrows(wo_d[l], m2 * 256, 256), [("wa", s, 0), ("wa", s, 1)], "wa%d_0" % s)
                    for mm in range(2):
                        m = m2 * 2 + mm
                        for t2 in range(2):
                            tq = th * 2 + t2
                            tsl = slice(tq * 512, (tq + 1) * 512)
                            usl = slice(t2 * 512, (t2 + 1) * 512)
                            by = y_rr.next()
                            xi = x_rr.next()
                            P.add("sp", lambda e, xi=xi, m=m, tsl=tsl: e.dma_start(out=xt[xi], in_=xT_d[m, :, tsl]),
                                  r=[("xT", m, tq)], w=[("xtp", xi)], dma="xtp%d" % xi)
                            mm_group(ps[by], [(wa[s][:, c, mm * 128:(mm + 1) * 128], uT[:, c, usl]) for c in range(NCH)],
                                     r=[("wa", s, 0), ("wa", s, 1)] + [("uT", c, t2) for c in range(NCH)], w=[("ps", by)])
                            P.add("dve", lambda e, by=by, xi=xi: e.tensor_tensor(out=xt[xi], in0=ps[by], in1=xt[xi], op=ALU.add),
                                  r=[("ps", by), ("xtp", xi)], w=[("xtp", xi)])
                            P.add("sp", lambda e, xi=xi, m=m, tsl=tsl: e.dma_start(out=xT_d[m, :, tsl], in_=xt[xi]),
                                  r=[("xtp", xi)], w=[("xT", m, tq)], dma="xtps%d" % xi)

        def final_phase(with_norm=True):
            P.barrier()
            arena.reset()
            xts = [arena.f32(NCH, 512) for _ in range(2)]
            yts = [arena.f32(NCH, 512) for _ in range(2)]
            sq = [arena.bf16(512) for _ in range(2)]
            rstds = [arena.f32(512) for _ in range(2)]
            ost = [arena.f32(D) for _ in range(2)]
            sq_rr = RR([(sq[0], "sq0"), (sq[1], "sq1")])
            ss_rr = RR([6, 7])
            b_rr = RR([0, 1, 2, 3, 4, 5])
            o_rr = RR([0, 1])

            def load_x(tq):
                bi = tq % 2
                for q4 in range(4):
                    P.add("sp", lambda e, bi=bi, q4=q4, tq=tq: e.dma_start(
                        out=xts[bi][:, q4 * 4:(q4 + 1) * 4, :],
                        in_=xT_d[q4 * 4:(q4 + 1) * 4, :, tq * 512:(tq + 1) * 512].rearrange("c p t -> p c t")),
                        r=[("xT", c, tq) for c in range(q4 * 4, q4 * 4 + 4)],
                        w=[("xf%d" % bi, c) for c in range(q4 * 4, q4 * 4 + 4)], dma="xtn%d_%d" % (bi, q4))
            load_x(0)
            for tq in range(4):
                bi = tq % 2
                xt, yt, rstd = xts[bi], yts[bi], rstds[bi]
                if tq + 1 < 4:
                    load_x(tq + 1)
                if with_norm:
                    rms_rstd(xt, 512, rstd, sq_rr, "xf%d" % bi, ss_rr, D)
                    for c in range(NCH):
                        P.add("dve", lambda e, c=c, xt=xt, yt=yt, rstd=rstd: e.scalar_tensor_tensor(
                            out=yt[:, c, :], in0=xt[:, c, :], scalar=cv[:, CV_FIN + c:CV_FIN + c + 1], in1=rstd,
                            op0=ALU.mult, op1=ALU.mult), r=[("xf%d" % bi, c), ("rstd", 0), "cv"], w=[("yf%d" % bi, c)])
                    src, sk = yt, "yf%d" % bi
                else:
                    src, sk = xt, "xf%d" % bi
                for tb in range(4):
                    oi = o_rr.next()
                    for q in range(4):
                        b = b_rr.next()

                        def ft(e, tb=tb, q=q, b=b, src=src):
                            ins = None
                            for j in range(4):
                                c = q * 4 + j
                                ins = e.transpose(out=ps[b][:, j * 128:(j + 1) * 128],
                                                  in_=src[:, c, tb * 128:(tb + 1) * 128], identity=ident)
                            return ins
                        P.add("pe", ft, r=[(sk, q * 4 + j) for j in range(4)] + ["ident"], w=[("ps", b)])
                        eng = "act" if q % 2 == 0 else "dve"

                        def fc(e, oi=oi, q=q, b=b, eng=eng):
                            o = ost[oi][:, q * 512:(q + 1) * 512]
                            if eng == "act":
                                return e.activation(out=o, in_=ps[b], func=AF.Copy)
                            return e.tensor_copy(out=o, in_=ps[b])
                        P.add(eng, fc, r=[("ps", b)], w=[("ost", oi, q)])
                    t0 = tq * 512 + tb * 128
                    P.add("sp", lambda e, oi=oi, t0=t0: e.dma_start(out=out_d[t0:t0 + 128, :], in_=ost[oi]),
                          r=[("ost", oi, q) for q in range(4)], w=[("OUT", seqi, t0)], dma="ost%d" % oi)

        for l in range(n_layers):
            if "ffn1" in do:
                ffn_phase(l, 0)
            if "mix" in do:
                mixer_norm(l)
                if "attn" in mix_sub:
                    mixer_attn(l)
                if "hgrn" in mix_sub:
                    mixer_hgrn(l)
                if "proj" in mix_sub:
                    mixer_proj(l)
            if "ffn2" in do:
                ffn_phase(l, 1)
        final_phase(with_norm=("nofinal" not in do))

    for sq in range(nseq):
        pipeline(x_all[sq], out_all[sq], xT_all[sq], sq)
    P.ops.append(("sp", None, tuple(("OUT", sq_, t0) for sq_ in range(nseq) for t0 in range(0, T, 128)) + ("phase",), ("END",), None))
    P.emit(dummy)
    return nc, P, arena


def make_consts(inputs):
    cvm = np.zeros((128, NCV), np.float32)
    fn = np.asarray(inputs["ffn_norm"], np.float32)
    cvm[:, CV_FFN:CV_FFN + 128] = fn.reshape(DEPTH, 2, NCH, 128).transpose(3, 0, 1, 2).reshape(128, -1)
    mn = np.asarray(inputs["mix_norm"], np.float32)
    cvm[:, CV_MIX:CV_MIX + 64] = mn.reshape(DEPTH, NCH, 128).transpose(2, 0, 1).reshape(128, -1)
    fin = np.asarray(inputs["final_norm"], np.float32)
    cvm[:, CV_FIN:CV_FIN + 16] = fin.reshape(NCH, 128).T
    bg = np.asarray(inputs["b_gate"], np.float32)
    cvm[:, CV_BG:CV_BG + 128] = bg.reshape(DEPTH, 32, 128).transpose(2, 0, 1).reshape(128, -1)
    lb = np.asarray(inputs["hgrn_lb"], np.float32)
    cvm[:, CV_LB:CV_LB + 32] = lb.reshape(DEPTH, 8, 128).transpose(2, 0, 1).reshape(128, -1)
    gn = np.asarray(inputs["hgrn_norm"], np.float32)
    cvm[:, CV_GN:CV_GN + 4] = gn.T
    p = np.arange(128)[:, None]
    f = np.arange(128)[None, :]
    cvm[:, CV_MASK:CV_MASK + 128] = (p <= f)
    cvm[:, CV_MASK + 128:CV_MASK + 256] = (p >= f)
    rst = np.ones((128, 512), np.float32)
    rst[:, 0::128] = 0.0
    cvm[:, CV_RST:CV_RST + 512] = rst
    return cvm, np.eye(128, dtype=np.float32)


_CACHE = {}


def run(inputs, n_layers=DEPTH, do=("ffn1", "mix", "ffn2"), mix_sub=("attn", "hgrn", "proj"), n_cores=N_CORES, trace=False, nseq=NSEQ):
    key = (n_layers, tuple(do), tuple(mix_sub), nseq)
    if key not in _CACHE:
        _CACHE[key] = build(n_layers, do, mix_sub, nseq)
    nc = _CACHE[key][0]
    cvm, ident = make_consts(inputs)
    x = np.ascontiguousarray(np.asarray(inputs["x"], np.float32))
    shared = {
        "ffn_w_in": np.ascontiguousarray(np.asarray(inputs["ffn_w_in"][:n_layers], np.float32)),
        "ffn_w_out": np.ascontiguousarray(np.asarray(inputs["ffn_w_out"][:n_layers], np.float32)),
        "w_in": np.ascontiguousarray(np.asarray(inputs["w_in"][:n_layers], np.float32)),
        "w_pa": np.ascontiguousarray(np.asarray(inputs["w_proj_attn"][:n_layers], np.float32)),
        "w_pm": np.ascontiguousarray(np.asarray(inputs["w_proj_hgrn"][:n_layers], np.float32)),
        "w_o": np.ascontiguousarray(np.asarray(inputs["w_out"][:n_layers], np.float32)),
        "cvec": cvm,
        "ident": ident,
    }
    in_maps = []
    for b in range(n_cores):
        m = dict(shared)
        m["x"] = np.ascontiguousarray(x[b * nseq:(b + 1) * nseq])
        in_maps.append(m)
    res = run_bass_kernel_spmd(nc, in_maps, core_ids=list(range(n_cores)), trace=trace)
    out = np.concatenate([np.asarray(r["out"], np.float32) for r in res.results], axis=0)
    return out, res


def kernel(**inputs):
    out, _ = run(inputs)
    return out
```

```python
import math
import numpy as np
import ml_dtypes
import concourse.bass as bass
import concourse.mybir as mybir
from concourse.bass_utils import run_bass_kernel_spmd

F32 = mybir.dt.float32
BF16 = mybir.dt.bfloat16
AF = mybir.ActivationFunctionType
ALU = mybir.AluOpType

D = 2048
T = 2048
DFF = 5504
NCH = D // 128
NJ = DFF // 128
DEPTH = 4
INW = 12800
EPS = 1e-6
N_CORES = 8
NSEQ = 1

OFF_QA, OFF_KA, OFF_VA = 0, 1536, 3072
OFF_QH, OFF_FH, OFF_IH, OFF_GH = 4608, 5632, 6656, 7680
OFF_GATE = 8704

CV_FFN = 0
CV_MIX = 128
CV_FIN = 192
CV_BG = 208
CV_LB = 336
CV_GN = 368
CV_MASK = 372
CV_RST = 628
NCV = 1140

SAME_SYNC = True
SEM_EPOCH = 20000


class Prog:
    ENGS = ("pe", "act", "dve", "pool", "sp")

    def __init__(self, nc):
        self.nc = nc
        self.ops = []
        self.cc_slots = set()

    def add(self, eng, fn, r=(), w=(), dma=None, free=False, cc=False):
        r = tuple(r)
        if not free:
            r = r + ("phase",)
        if cc:
            self.cc_slots.add(dma)
        self.ops.append((eng, fn, r, tuple(w), dma))

    def barrier(self):
        self.ops.append(("pool", None, (), ("phase",), None))

    def emit(self, dummy):
        nc = self.nc
        ops = self.ops
        n = len(ops)
        last_w = {}
        readers = {}
        deps = [None] * n
        for i, (eng, fn, r, w, dma) in enumerate(ops):
            d = set()
            for k in r:
                j = last_w.get(k)
                if j is not None:
                    d.add(j)
            for k in w:
                j = last_w.get(k)
                if j is not None:
                    d.add(j)
                rd = readers.get(k)
                if rd:
                    d.update(rd.values())
            d.discard(i)
            deps[i] = d
            src = (eng, dma) if dma else eng
            for k in r:
                readers.setdefault(k, {})[src] = i
            for k in w:
                last_w[k] = i
                readers[k] = {}
        need = [False] * n
        fdeps = [None] * n
        for i in range(n):
            eng = ops[i][0]
            lst = []
            for j in deps[i]:
                ej, _, _, _, dj = ops[j]
                if dj is None and ej == eng:
                    if eng == "pe" or eng == "sp" or not SAME_SYNC:
                        continue
                need[j] = True
                lst.append(j)
            lst.sort()
            fdeps[i] = lst
        cnt = {e: 0 for e in self.ENGS}
        slot_cnt = {}
        sig = [None] * n
        for i in range(n):
            eng, fn, r, w, dma = ops[i]
            if dma:
                slot_cnt[dma] = slot_cnt.get(dma, 0) + 1
                sig[i] = (("dma", dma), (1 if dma in self.cc_slots else 16) * slot_cnt[dma])
            elif need[i]:
                c = cnt[eng]
                cnt[eng] = c + 1
                sig[i] = (("eng", eng, c // SEM_EPOCH), c % SEM_EPOCH + 1)
        sems = {}
        for s in sig:
            if s is not None and s[0] not in sems:
                sems[s[0]] = nc.alloc_semaphore("s_" + "_".join(str(t) for t in s[0]))
        self.n_sems = len(sems)
        by_eng = {e: [] for e in self.ENGS}
        for i in range(n):
            by_eng[ops[i][0]].append(i)
        waited = {}
        self.n_wait = 0

        def run(engname, e):
            for i in by_eng[engname]:
                eng, fn, r, w, dma = ops[i]
                for j in fdeps[i]:
                    sk, val = sig[j]
                    key = (engname, sk)
                    if waited.get(key, 0) >= val:
                        continue
                    if sk[0] == "eng":
                        later = [kk for kk in waited if kk[0] == engname and kk[1][0] == "eng"
                                 and kk[1][1] == sk[1] and kk[1][2] > sk[2]]
                        if later:
                            continue
                    waited[key] = val
                    e.wait_ge(sems[sk], val)
                    self.n_wait += 1
                if fn is None:
                    ins = e.nop()
                else:
                    ins = fn(e)
                if sig[i] is not None:
                    if dma in self.cc_slots:
                        ins.then_inc(sems[sig[i][0]])
                    else:
                        ins.then_inc(sems[sig[i][0]], 16 if dma else 1)

        with nc.Block() as block:
            @block.tensor
            def _(e):
                run("pe", e)

            @block.scalar
            def _(e):
                run("act", e)

            @block.vector
            def _(e):
                run("dve", e)

            @block.gpsimd
            def _(e):
                run("pool", e)

            @block.sync
            def _(e):
                run("sp", e)


class Arena:
    def __init__(self, ap, nwords):
        self.ap = ap
        self.n = nwords
        self.off = 0
        self.peak = 0

    def reset(self):
        self.off = 0

    def _take(self, nw):
        a = self.ap[:, self.off:self.off + nw]
        self.off += nw
        self.peak = max(self.peak, self.off)
        assert self.off <= self.n, ("arena overflow", self.off, self.n)
        return a

    def f32(self, *shape):
        n = int(np.prod(shape))
        a = self._take(n)
        if len(shape) == 2:
            a = a.rearrange("p (a b) -> p a b", a=shape[0])
        return a

    def bf16(self, *shape):
        n = int(np.prod(shape))
        a = self._take((n + 1) // 2).bitcast(BF16)
        if len(shape) == 2:
            a = a.rearrange("p (a b) -> p a b", a=shape[0])
        return a


class RR:
    def __init__(self, items):
        self.items = list(items)
        self.i = 0

    def next(self):
        v = self.items[self.i % len(self.items)]
        self.i += 1
        return v


def build(n_layers=DEPTH, do=("ffn1", "mix", "ffn2"), mix_sub=("attn", "hgrn", "proj"), nseq=1):
    nc = bass.Bass("TRN2", target_bir_lowering=False)
    P = Prog(nc)

    x_all = nc.dram_tensor("x", [nseq, T, D], F32, kind="ExternalInput").ap()
    WSPEC = {
        "ffn_w_in": ((n_layers, 2), D, 2 * DFF, 32),
        "ffn_w_out": ((n_layers, 2), DFF, D, 172),
        "w_in": ((n_layers,), D, INW, 32),
        "w_pa": ((n_layers,), 512, D, 64),
        "w_pm": ((n_layers,), 1024, D, 128),
        "w_o": ((n_layers,), D, D, 128),
    }
    fwi_d = nc.dram_tensor("ffn_w_in", [n_layers, 2, D, 2 * DFF], F32, kind="ExternalInput").ap()
    fwo_d = nc.dram_tensor("ffn_w_out", [n_layers, 2, DFF, D], F32, kind="ExternalInput").ap()
    win_d = nc.dram_tensor("w_in", [n_layers, D, INW], F32, kind="ExternalInput").ap()
    wpa_d = nc.dram_tensor("w_pa", [n_layers, 512, D], F32, kind="ExternalInput").ap()
    wpm_d = nc.dram_tensor("w_pm", [n_layers, 1024, D], F32, kind="ExternalInput").ap()
    wo_d = nc.dram_tensor("w_o", [n_layers, D, D], F32, kind="ExternalInput").ap()

    def wkeys(nm, idx, r0=None, r1=None):
        lead, rtot, ccols, rp = WSPEC[nm]
        npc = rtot // (8 * rp)
        if r0 is None:
            ks = range(npc)
        else:
            ks = range(r0 // (8 * rp), (r1 - 1) // (8 * rp) + 1)
        return [("wf", nm, idx, k) for k in ks]

    cv_d = nc.dram_tensor("cvec", [128, NCV], F32, kind="ExternalInput").ap()
    id_d = nc.dram_tensor("ident", [128, 128], F32, kind="ExternalInput").ap()
    out_all = nc.dram_tensor("out", [nseq, T, D], F32, kind="ExternalOutput").ap()
    xT_all = nc.dram_tensor("xT_scr", [nseq, NCH, 128, T], F32, kind="Internal").ap()

    def sb(name, shape, dt):
        return nc.alloc_sbuf_tensor(name, shape, dt).ap()

    cv = sb("cv", [128, NCV], F32)
    ident = sb("ident_sb", [128, 128], F32)
    identb = sb("identb", [128, 128], BF16)
    ones32 = sb("ones32", [128, 128], F32)
    onesb = sb("onesb", [128, 128], BF16)
    maskb = sb("maskb", [128, 256], BF16)
    epst = sb("epst", [128, 1], F32)
    lbt = sb("lbt", [128, 32], F32)
    omlt = sb("omlt", [128, 32], F32)
    lbtmp = sb("lbtmp", [128, 48], F32)
    dummy = sb("dmy0", [128, 1], F32)
    WA_N = 4
    wa = [sb("wa%d" % i, [128, 16, 256], BF16) for i in range(WA_N)]
    ARENA_WORDS = 42400
    arena = Arena(sb("arena", [128, ARENA_WORDS], F32), ARENA_WORDS)
    ps = [nc.alloc_psum_tensor("ps%d" % i, [128, 512], F32).ap() for i in range(8)]

    wa_rr = RR(range(WA_N))

    P.add("sp", lambda e: e.dma_start(out=cv, in_=cv_d), w=["cv"], dma="cv", free=True)
    P.add("sp", lambda e: e.dma_start(out=ident, in_=id_d), w=["ident"], dma="ident", free=True)
    P.add("dve", lambda e: e.memset(ones32, 1.0), w=["ones32"], free=True)
    P.add("dve", lambda e: e.memset(onesb, 1.0), w=["onesb"], free=True)
    P.add("dve", lambda e: e.memset(epst, EPS), w=["epst"], free=True)
    P.add("dve", lambda e: e.tensor_copy(out=maskb, in_=cv[:, CV_MASK:CV_MASK + 256]),
          r=["cv"], w=["maskb"], free=True)
    P.add("dve", lambda e: e.tensor_copy(out=identb, in_=ident), r=["ident"], w=["identb"], free=True)
    ex = lbtmp[:, 0:32]
    ssum = lbtmp[:, 32:40]
    rs = lbtmp[:, 40:48]
    P.add("act", lambda e: e.activation(out=ex, in_=cv[:, CV_LB:CV_LB + 32], func=AF.Exp),
          r=["cv"], w=["lb_ex"], free=True)
    P.add("dve", lambda e: e.tensor_tensor(out=ssum, in0=ex[:, 0:8], in1=ex[:, 8:16], op=ALU.add),
          r=["lb_ex"], w=["lb_s"], free=True)
    P.add("dve", lambda e: e.tensor_tensor(out=ssum, in0=ssum, in1=ex[:, 16:24], op=ALU.add),
          r=["lb_ex", "lb_s"], w=["lb_s"], free=True)
    P.add("dve", lambda e: e.tensor_tensor(out=ssum, in0=ssum, in1=ex[:, 24:32], op=ALU.add),
          r=["lb_ex", "lb_s"], w=["lb_s"], free=True)
    P.add("dve", lambda e: e.reciprocal(out=rs, in_=ssum), r=["lb_s"], w=["lb_r"], free=True)
    P.add("dve", lambda e: e.memset(lbt[:, 0:8], 0.0), w=["lbt0"], free=True)
    for l in range(1, DEPTH):
        def f(e, l=l):
            return e.tensor_tensor(out=lbt[:, 8 * l:8 * l + 8], in0=ex[:, 8 * l:8 * l + 8], in1=rs, op=ALU.mult)
        P.add("dve", f, r=["lb_ex", "lb_r"], w=["lbt%d" % l], free=True)
    for l in range(2, DEPTH):
        def f(e, l=l):
            return e.tensor_tensor(out=lbt[:, 8 * l:8 * l + 8], in0=lbt[:, 8 * l:8 * l + 8],
                                   in1=lbt[:, 8 * l - 8:8 * l], op=ALU.add)
        P.add("dve", f, r=["lbt%d" % l, "lbt%d" % (l - 1)], w=["lbt%d" % l], free=True)
    P.add("dve", lambda e: e.tensor_scalar(out=omlt, in0=lbt, scalar1=-1.0, scalar2=1.0,
                                            op0=ALU.mult, op1=ALU.add),
          r=["lbt%d" % l for l in range(DEPTH)], w=["omlt"], free=True)
    LBK = ["lbt%d" % l for l in range(DEPTH)] + ["omlt"]

    def load_w(slot_ap, src_ap, key, slotname, rk=()):
        keys = key if isinstance(key, list) else [key]
        P.add("pool", lambda e: e.dma_start(out=slot_ap, in_=src_ap), r=list(rk), w=keys, dma=slotname, free=True)

    def wrows(w2d, c0, ncols):
        return w2d[:, c0:c0 + ncols].rearrange("(c p) n -> p c n", p=128)

    def mm_group(out_ps, pairs, r, w):
        def fn(e):
            ins = None
            n = len(pairs)
            for idx, (l, rh) in enumerate(pairs):
                ins = e.matmul(out_ps, lhsT=l, rhs=rh, start=(idx == 0), stop=(idx == n - 1))
            return ins
        P.add("pe", fn, r=r, w=w)

    def rms_rstd(x_tile, ntok, rstd, sq_bufs, tagx, ss_banks, nfeat, after_tt=None):
        for tt in range(ntok // 512):
            b = ss_banks.next()
            for c in range(NCH):
                s = sq_bufs.next()
                sqt, sqk = s

                def f1(e, c=c, tt=tt, sqt=sqt):
                    return e.activation(out=sqt, in_=x_tile[:, c, tt * 512:(tt + 1) * 512], func=AF.Square)
                P.add("act", f1, r=[(tagx, c)], w=[sqk])

                def f2(e, c=c, sqt=sqt, b=b):
                    return e.matmul(ps[b], lhsT=onesb, rhs=sqt, start=(c == 0), stop=(c == NCH - 1))
                P.add("pe", f2, r=[sqk, "onesb"] + ([("ps", b)] if c > 0 else []), w=[("ps", b)])

            def f3(e, tt=tt, b=b):
                return e.activation(out=rstd[:, tt * 512:(tt + 1) * 512], in_=ps[b], func=AF.Sqrt,
                                    bias=epst, scale=1.0 / nfeat)
            P.add("act", f3, r=[("ps", b), "epst"], w=[("rstd", tt)])

            def f4(e, tt=tt):
                return e.reciprocal(out=rstd[:, tt * 512:(tt + 1) * 512], in_=rstd[:, tt * 512:(tt + 1) * 512])
            P.add("dve", f4, r=[("rstd", tt)], w=[("rstd", tt)])
            if after_tt is not None:
                after_tt(tt)

    def pipeline(x_d, out_d, xT_d, seqi):
        P.barrier()
        arena.reset()
        xin = [arena.f32(D) for _ in range(3)]
        xst = [arena.f32(NCH, 128) for _ in range(2)]
        bank_rr = RR(range(8))

        def load_tile(t16):
            si = t16 % 3
            P.add("sp", lambda e, si=si, t16=t16: e.dma_start(out=xin[si], in_=x_d[t16 * 128:(t16 + 1) * 128, :]),
                  w=[("xin", si)], dma="xin%d" % si)
        load_tile(0)
        load_tile(1)
        for t16 in range(T // 128):
            s = t16 % 3
            s2 = t16 % 2
            for q in range(4):
                b = bank_rr.next()

                def ft(e, s=s, q=q, b=b):
                    ins = None
                    for j in range(4):
                        c = q * 4 + j
                        ins = e.transpose(out=ps[b][:, j * 128:(j + 1) * 128], in_=xin[s][:, c * 128:(c + 1) * 128],
                                          identity=ident)
                    return ins
                P.add("pe", ft, r=[("xin", s), "ident"], w=[("ps", b)])
                eng = "act" if q % 2 == 0 else "dve"

                def fc(e, s2=s2, q=q, b=b, eng=eng):
                    o = xst[s2][:, q * 4:(q + 1) * 4, :]
                    i = ps[b].rearrange("p (a b) -> p a b", a=4)
                    if eng == "act":
                        return e.activation(out=o, in_=i, func=AF.Copy)
                    return e.tensor_copy(out=o, in_=i)
                P.add(eng, fc, r=[("ps", b)], w=[("xst", s2, q)])
            P.add("sp", lambda e, s2=s2, t16=t16: e.dma_start(
                out=xT_d[:, :, t16 * 128:(t16 + 1) * 128].rearrange("c p t -> p c t"), in_=xst[s2]),
                r=[("xst", s2, q) for q in range(4)], w=[("xT", c, t16 // 4) for c in range(NCH)], dma="xst%d" % s2)
            if t16 + 2 < T // 128:
                load_tile(t16 + 2)

        def ffn_phase(l, i):
            P.barrier()
            arena.reset()
            x32 = arena.f32(NCH, 1024)
            hT = arena.bf16(NCH, 1024)
            gT = [arena.bf16(4, 1024) for _ in range(2)]
            wb = [arena.bf16(4, 2048) for _ in range(2)]
            sq = [arena.bf16(512) for _ in range(2)]
            rstd = arena.f32(1024)
            sg = [arena.f32(512) for _ in range(2)]
            w_in = fwi_d[l, i]
            w_out = fwo_d[l, i]
            gcol = CV_FFN + (l * 2 + i) * NCH
            gu_rr = RR([(0, 1), (2, 3)])
            y_rr = RR([4, 5, 6, 7])
            ss_rr = RR([6, 7])
            sq_rr = RR([(sq[0], "sq0"), (sq[1], "sq1")])
            sg_rr = RR([0, 1])
            g_rr = RR([0, 1])
            wb_rr = RR([0, 1])
            groups = []
            j = 0
            while j < NJ:
                groups.append(list(range(j, min(j + 4, NJ))))
                j += 4
            for half in range(2):
                T0 = half * 1024
                for c in range(NCH):
                    P.add("sp", lambda e, c=c, T0=T0: e.dma_start(out=x32[:, c, :], in_=xT_d[c, :, T0:T0 + 1024]),
                          r=[("xT", c, 2 * half), ("xT", c, 2 * half + 1)], w=[("x32", c)], dma="x32_%d" % c)
                def emit_h(tt):
                    for c in range(NCH):
                        def fh(e, c=c, tt=tt):
                            return e.scalar_tensor_tensor(
                                out=hT[:, c, tt * 512:(tt + 1) * 512], in0=x32[:, c, tt * 512:(tt + 1) * 512],
                                scalar=cv[:, gcol + c:gcol + c + 1], in1=rstd[:, tt * 512:(tt + 1) * 512],
                                op0=ALU.mult, op1=ALU.mult)
                        P.add("dve", fh, r=[("x32", c), ("rstd", tt), "cv"], w=[("hT", c, tt)])
                rms_rstd(x32, 1024, rstd, sq_rr, "x32", ss_rr, D, after_tt=emit_h)
                for gi, grp in enumerate(groups):
                    gs = g_rr.next()
                    slots_ = {}
                    for jj, j in enumerate(grp):
                        s = wa_rr.next()
                        slots_[jj] = s
                        load_w(wa[s][:, :, 0:128], wrows(w_in, j * 128, 128), ("wa", s, 0), "wa%d_0" % s, wkeys("ffn_w_in", (l, i)))
                        load_w(wa[s][:, :, 128:256], wrows(w_in, DFF + j * 128, 128), ("wa", s, 1), "wa%d_1" % s, wkeys("ffn_w_in", (l, i)))
                    if gi == 0:
                        order = [(jj, tt) for tt in range(2) for jj in range(len(grp))]
                    else:
                        order = [(jj, tt) for jj in range(len(grp)) for tt in range(2)]
                    for jj, tt in order:
                        s = slots_[jj]
                        bg, bu = gu_rr.next()
                        hk = [("hT", c, tt) for c in range(NCH)]
                        mm_group(ps[bg], [(wa[s][:, c, 0:128], hT[:, c, tt * 512:(tt + 1) * 512]) for c in range(NCH)],
                                 r=[("wa", s, 0)] + hk, w=[("ps", bg)])
                        mm_group(ps[bu], [(wa[s][:, c, 128:256], hT[:, c, tt * 512:(tt + 1) * 512]) for c in range(NCH)],
                                 r=[("wa", s, 1)] + hk, w=[("ps", bu)])
                        si = sg_rr.next()
                        P.add("act", lambda e, si=si, bg=bg: e.activation(out=sg[si], in_=ps[bg], func=AF.Silu),
                              r=[("ps", bg)], w=[("sg", si)])
                        P.add("dve", lambda e, si=si, bu=bu, gs=gs, jj=jj, tt=tt: e.tensor_tensor(
                            out=gT[gs][:, jj, tt * 512:(tt + 1) * 512], in0=sg[si], in1=ps[bu], op=ALU.mult),
                            r=[("sg", si), ("ps", bu)], w=[("gT", gs, jj, tt)])
                    ws = wb_rr.next()
                    G = len(grp)
                    j0 = grp[0]
                    P.add("pool", lambda e, ws=ws, G=G, j0=j0: e.dma_start(
                        out=wb[ws][:, 0:G, :], in_=w_out[j0 * 128:(j0 + G) * 128, :].rearrange("(g p) n -> p g n", p=128)),
                        r=wkeys("ffn_w_out", (l, i), j0 * 128, (j0 + G) * 128), w=[("wb", ws)], dma="wb%d" % ws)
                    for m in range(NCH):
                        for tt in range(2):
                            by = y_rr.next()
                            mm_group(ps[by], [(wb[ws][:, jj, m * 128:(m + 1) * 128], gT[gs][:, jj, tt * 512:(tt + 1) * 512])
                                              for jj in range(G)],
                                     r=[("wb", ws)] + [("gT", gs, jj, tt) for jj in range(G)], w=[("ps", by)])
                            P.add("dve", lambda e, by=by, m=m, tt=tt: e.scalar_tensor_tensor(
                                out=x32[:, m, tt * 512:(tt + 1) * 512], in0=ps[by], scalar=0.5,
                                in1=x32[:, m, tt * 512:(tt + 1) * 512], op0=ALU.mult, op1=ALU.add),
                                r=[("ps", by), ("x32", m)], w=[("x32", m)])
                for c in range(NCH):
                    P.add("sp", lambda e, c=c, T0=T0: e.dma_start(out=xT_d[c, :, T0:T0 + 1024], in_=x32[:, c, :]),
                          r=[("x32", c)], w=[("xT", c, 2 * half), ("xT", c, 2 * half + 1)], dma="x32s_%d" % c)

        def mixer_fixed():
            arena.reset()
            hT = arena.bf16(NCH, T)
            aT = arena.bf16(4, T)
            mT = arena.bf16(8, T)
            return hT, aT, mT

        def mixer_norm(l):
            P.barrier()
            hT, aT, mT = mixer_fixed()
            TW = 256
            xts = [arena.f32(NCH, TW) for _ in range(2)]
            sq = [arena.bf16(TW) for _ in range(2)]
            rstds = [arena.f32(TW) for _ in range(2)]
            gcol = CV_MIX + l * NCH
            sq_rr = RR([0, 1])
            for ti in range(T // TW):
                bi = ti % 2
                xt = xts[bi]
                rstd = rstds[bi]
                tq = (ti * TW) // 512
                t0 = ti * TW
                b = 6 + bi
                for q4 in range(4):
                    P.add("sp", lambda e, t0=t0, xt=xt, q4=q4: e.dma_start(
                        out=xt[:, q4 * 4:(q4 + 1) * 4, :],
                        in_=xT_d[q4 * 4:(q4 + 1) * 4, :, t0:t0 + TW].rearrange("c p t -> p c t")),
                        r=[("xT", c, tq) for c in range(q4 * 4, q4 * 4 + 4)],
                        w=[("xtn", bi, c) for c in range(q4 * 4, q4 * 4 + 4)], dma="xtn%d_%d" % (bi, q4))
                for c in range(NCH):
                    si = sq_rr.next()
                    P.add("act", lambda e, c=c, si=si, xt=xt: e.activation(out=sq[si], in_=xt[:, c, :], func=AF.Square),
                          r=[("xtn", bi, c)], w=[("sqn", si)])
                    P.add("pe", lambda e, c=c, si=si, b=b: e.matmul(ps[b][:, 0:TW], lhsT=onesb, rhs=sq[si],
                                                                     start=(c == 0), stop=(c == NCH - 1)),
                          r=[("sqn", si), "onesb"] + ([("ps", b)] if c > 0 else []), w=[("ps", b)])
                P.add("act", lambda e, b=b, rstd=rstd: e.activation(out=rstd, in_=ps[b][:, 0:TW], func=AF.Sqrt,
                                                                     bias=epst, scale=1.0 / D),
                      r=[("ps", b), "epst"], w=[("rstdn", bi)])
                P.add("dve", lambda e, rstd=rstd: e.reciprocal(out=rstd, in_=rstd), r=[("rstdn", bi)], w=[("rstdn", bi)])
                for c in range(NCH):
                    P.add("dve", lambda e, c=c, t0=t0, xt=xt, rstd=rstd: e.scalar_tensor_tensor(
                        out=hT[:, c, t0:t0 + TW], in0=xt[:, c, :], scalar=cv[:, gcol + c:gcol + c + 1], in1=rstd,
                        op0=ALU.mult, op1=ALU.mult),
                        r=[("xtn", bi, c), ("rstdn", bi), "cv"], w=[("hT", c, tq)])

        def proj_cols(hT, w2d, col0, slot, half, banks, tts=range(4)):
            for tt in tts:
                b = banks[tt]
                mm_group(ps[b], [(wa[slot][:, c, half * 128:(half + 1) * 128], hT[:, c, tt * 512:(tt + 1) * 512])
                                 for c in range(NCH)],
                         r=[("wa", slot, half)] + [("hT", c, tt) for c in range(NCH)], w=[("ps", b)])

        def mixer_attn(l):
            P.barrier()
            hT, aT, mT = mixer_fixed()
            num = arena.f32(T)
            den = arena.f32(T)
            qkv = [[arena.bf16(T) for _ in range(3)] for _ in range(2)]
            vtok = [arena.bf16(16, 128) for _ in range(2)]
            E = [arena.bf16(256) for _ in range(8)]
            w2d = win_d[l]
            set_rr = RR([0, 1])
            e_rr = RR(range(8))
            scale = 1.0 / math.sqrt(128.0)
            evac_rr = RR(["act", "dve"])
            for h in range(4):
                for g, r_ in enumerate((1, 4, 16)):
                    L = T // r_
                    nbs = L // 128
                    st = set_rr.next()
                    qT, kT, vT = qkv[st]
                    s0 = wa_rr.next()
                    load_w(wa[s0][:, :, 0:128], wrows(w2d, OFF_QA + g * 512 + h * 128, 128), ("wa", s0, 0), "wa%d_0" % s0, wkeys("w_in", (l,)))
                    load_w(wa[s0][:, :, 128:256], wrows(w2d, OFF_KA + g * 512 + h * 128, 128), ("wa", s0, 1), "wa%d_1" % s0, wkeys("w_in", (l,)))
                    s1 = wa_rr.next()
                    load_w(wa[s1][:, :, 0:128], wrows(w2d, OFF_VA + g * 512 + h * 128, 128), ("wa", s1, 0), "wa%d_0" % s1, wkeys("w_in", (l,)))
                    for (slot, hf, dst, nm) in ((s0, 0, qT, "q"), (s0, 1, kT, "k"), (s1, 0, vT, "v")):
                        banks = [0, 1, 2, 3] if nm != "k" else [4, 5, 6, 7]
                        proj_cols(hT, w2d, 0, slot, hf, banks)
                        for tt in range(4):
                            eng = evac_rr.next()

                            def fe(e, dst=dst, tt=tt, b=banks[tt], eng=eng):
                                if eng == "act":
                                    return e.activation(out=dst[:, tt * 512:(tt + 1) * 512], in_=ps[b], func=AF.Copy)
                                return e.tensor_copy(out=dst[:, tt * 512:(tt + 1) * 512], in_=ps[b])
                            P.add(eng, fe, r=[("ps", banks[tt])], w=[(nm, st, tt)])
                    qv = qT.rearrange("p (m r) -> p r m", r=r_)
                    kv = kT.rearrange("p (m r) -> p r m", r=r_)
                    vv = vT.rearrange("p (m r) -> p r m", r=r_)
                    nv = num.rearrange("p (m r) -> p r m", r=r_)
                    dv = den.rearrange("p (m r) -> p r m", r=r_)
                    allq = [("q", st, tt) for tt in range(4)]
                    allk = [("k", st, tt) for tt in range(4)]
                    allv = [("v", st, tt) for tt in range(4)]
                    for half8 in range(2):
                        b = 0 + half8
                        pb = ps[b].bitcast(BF16)

                        def ftr(e, half8=half8, pb=pb, vv=vv, nbs=nbs):
                            ins = None
                            for k in range(8):
                                B = half8 * 8 + k
                                c_, n_ = B // nbs, B % nbs
                                ins = e.transpose(out=pb[:, k * 128:(k + 1) * 128],
                                                  in_=vv[:, c_, n_ * 128:(n_ + 1) * 128], identity=identb)
                            return ins
                        P.add("pe", ftr, r=allv + ["identb"], w=[("ps", b)])
                        P.add("dve", lambda e, half8=half8, pb=pb, st=st: e.tensor_copy(
                            out=vtok[st][:, half8 * 8:(half8 + 1) * 8, :], in_=pb.rearrange("p (a b) -> p a b", a=8)),
                            r=[("ps", b)], w=[("vtok", st, half8)])
                    Eof = {}
                    for B in range(16):
                        c_, n_ = B // nbs, B % nbs
                        wid = 256 if n_ + 1 < nbs else 128
                        b = 2 + (B % 2)
                        ei = e_rr.next()
                        Eof[B] = ei

                        def fs(e, c_=c_, n_=n_, wid=wid, b=b, kv=kv, qv=qv):
                            return e.matmul(ps[b][:, 0:wid], lhsT=kv[:, c_, n_ * 128:(n_ + 1) * 128],
                                            rhs=qv[:, c_, n_ * 128:n_ * 128 + wid], start=True, stop=True)
                        P.add("pe", fs, r=allq + allk, w=[("ps", b)])
                        P.add("act", lambda e, b=b, wid=wid, ei=ei: e.activation(
                            out=E[ei][:, 0:wid], in_=ps[b][:, 0:wid], func=AF.Exp, scale=scale),
                            r=[("ps", b)], w=[("E", ei)])
                        P.add("dve", lambda e, wid=wid, ei=ei: e.tensor_tensor(
                            out=E[ei][:, 0:wid], in0=E[ei][:, 0:wid], in1=maskb[:, 0:wid], op=ALU.mult),
                            r=[("E", ei), "maskb"], w=[("E", ei)])
                        if B % 4 == 3:
                            B0 = B - 3
                            bo, bd = 4 + ((B // 4) % 2) * 2, 5 + ((B // 4) % 2) * 2

                            def fo(e, B0=B0, bo=bo, bd=bd, nbs=nbs, st=st, Eof=dict(Eof)):
                                ins = None
                                for lhs_kind, bank in (("v", bo), ("1", bd)):
                                    for k in range(4):
                                        Bq = B0 + k
                                        nq = Bq % nbs
                                        o = ps[bank][:, k * 128:(k + 1) * 128]
                                        has_prev = nq > 0
                                        if has_prev:
                                            lp = vtok[st][:, Bq - 1, :] if lhs_kind == "v" else onesb
                                            ins = e.matmul(o, lhsT=lp, rhs=E[Eof[Bq - 1]][:, 128:256], start=True, stop=False)
                                        lc = vtok[st][:, Bq, :] if lhs_kind == "v" else onesb
                                        ins = e.matmul(o, lhsT=lc, rhs=E[Eof[Bq]][:, 0:128], start=(not has_prev), stop=True)
                                return ins
                            need_e = [("E", Eof[bb]) for bb in range(max(B0 - 1, 0), B + 1)]
                            P.add("pe", fo, r=need_e + [("vtok", st, 0), ("vtok", st, 1), "onesb"],
                                  w=[("ps", bo), ("ps", bd)])
                            if r_ == 1:
                                no = nv[:, 0, B0 * 128:(B0 + 4) * 128]
                                do_ = dv[:, 0, B0 * 128:(B0 + 4) * 128]
                                pso, psd = ps[bo], ps[bd]
                            elif r_ == 4:
                                no = nv[:, B0 // 4, :]
                                do_ = dv[:, B0 // 4, :]
                                pso, psd = ps[bo], ps[bd]
                            else:
                                no = nv[:, B0:B0 + 4, :]
                                do_ = dv[:, B0:B0 + 4, :]
                                pso = ps[bo].rearrange("p (a b) -> p a b", a=4)
                                psd = ps[bd].rearrange("p (a b) -> p a b", a=4)
                            if g == 0:
                                P.add("act", lambda e, no=no, pso=pso: e.activation(out=no, in_=pso, func=AF.Copy),
                                      r=[("ps", bo)], w=[("num", B0 // 4)])
                                P.add("dve", lambda e, do_=do_, psd=psd: e.tensor_copy(out=do_, in_=psd),
                                      r=[("ps", bd)], w=[("den", B0 // 4)])
                            else:
                                allnum = [("num", k) for k in range(4)]
                                allden = [("den", k) for k in range(4)]
                                P.add("dve", lambda e, no=no, pso=pso: e.tensor_tensor(out=no, in0=pso, in1=no, op=ALU.add),
                                      r=[("ps", bo)] + allnum, w=allnum)
                                P.add("dve", lambda e, do_=do_, psd=psd: e.tensor_tensor(out=do_, in0=psd, in1=do_, op=ALU.add),
                                      r=[("ps", bd)] + allden, w=allden)
                allnum = [("num", k) for k in range(4)]
                allden = [("den", k) for k in range(4)]
                P.add("dve", lambda e: e.reciprocal(out=den, in_=den), r=allden, w=allden)
                P.add("dve", lambda e, h=h: e.tensor_tensor(out=aT[:, h, :], in0=num, in1=den, op=ALU.mult),
                      r=allnum + allden, w=[("aT", h)])

        def mixer_hgrn(l):
            P.barrier()
            hT, aT, mT = mixer_fixed()
            w2d = win_d[l]
            tf = arena.f32(512)
            tlf = arena.f32(512)
            tb = arena.f32(512)
            td1 = arena.f32(512)
            td2 = arena.f32(512)
            te1 = arena.f32(512)
            teb = arena.f32(512)
            tqs = arena.f32(512)
            tgss = [arena.f32(512) for _ in range(2)]
            pending = [None]
            tosb = arena.f32(512)
            tsq = arena.bf16(512)
            trr = arena.f32(512)
            qt_ = arena.bf16(512)
            qh_ = arena.bf16(512)
            kt_ = arena.bf16(512)
            kh_ = arena.bf16(512)
            vT_ = arena.bf16(512)
            tok = arena.bf16(8, 128)
            A4 = arena.bf16(4, 128)
            S = arena.f32(128)
            Sbs = [arena.bf16(4, 128) for _ in range(2)]
            Sbz = arena.bf16(128)
            ebl = arena.f32(4)
            P.add("dve", lambda e: e.memset(A4, 0.0), w=["A4"])
            P.add("dve", lambda e: e.memset(Sbz, 0.0), w=["Sbz"])
            stepi = [0]
            lcol = l * 8
            a_rr = RR([0, 1])
            hslots = {}

            def emit_proj(hh, tt_, pieces):
                if hh not in hslots:
                    s0 = wa_rr.next()
                    load_w(wa[s0][:, :, 0:128], wrows(w2d, OFF_QH + hh * 128, 128), ("wa", s0, 0), "wa%d_0" % s0)
                    load_w(wa[s0][:, :, 128:256], wrows(w2d, OFF_FH + hh * 128, 128), ("wa", s0, 1), "wa%d_1" % s0)
                    s1 = wa_rr.next()
                    load_w(wa[s1][:, :, 0:128], wrows(w2d, OFF_IH + hh * 128, 128), ("wa", s1, 0), "wa%d_0" % s1)
                    load_w(wa[s1][:, :, 128:256], wrows(w2d, OFF_GH + hh * 128, 128), ("wa", s1, 1), "wa%d_1" % s1)
                    hslots[hh] = (s0, s1)
                s0, s1 = hslots[hh]
                for p_ in pieces:
                    slot, half = ((s0, 0), (s0, 1), (s1, 0), (s1, 1))[p_]
                    proj_cols(hT, w2d, 0, slot, half, {tt_: p_}, tts=[tt_])

            for h in range(8):
                P.add("dve", lambda e: e.memset(S, 0.0), w=["S"])
                lb_ap = lbt[:, lcol + h:lcol + h + 1]
                oml_ap = omlt[:, lcol + h:lcol + h + 1]
                for tt in range(4):
                    if (h, tt) == (0, 0):
                        emit_proj(0, 0, [0, 1, 2, 3])
                    nxt = (h, tt + 1) if tt < 3 else ((h + 1, 0) if h < 7 else None)
                    b3 = lambda a: a.rearrange("p (a b) -> p a b", a=4)
                    kpar = (h * 4 + tt) % 2
                    tgs = tgss[kpar]
                    P.add("act", lambda e: e.activation(out=tf, in_=ps[1], func=AF.Sigmoid), r=[("ps", 1)], w=["tf"])
                    P.add("act", lambda e: e.activation(out=tqs, in_=ps[0], func=AF.Silu), r=[("ps", 0)], w=["tqs"])
                    P.add("act", lambda e, tgs=tgs: e.activation(out=tgs, in_=ps[3], func=AF.Silu), r=[("ps", 3)], w=[("tgs", kpar)])
                    P.add("act", lambda e: e.activation(out=vT_, in_=ps[2], func=AF.Copy), r=[("ps", 2)], w=["vT"])
                    if nxt is not None:
                        emit_proj(nxt[0], nxt[1], [0, 1])
                    P.add("dve", lambda e, oml_ap=oml_ap, lb_ap=lb_ap: e.tensor_scalar(
                        out=tf, in0=tf, scalar1=oml_ap, scalar2=lb_ap, op0=ALU.mult, op1=ALU.add),
                        r=["tf"] + LBK, w=["tf"])
                    P.add("act", lambda e: e.activation(out=tlf, in_=tf, func=AF.Ln), r=["tf"], w=["tlf"])
                    P.add("dve", lambda e: e.tensor_scalar(out=tf, in0=tf, scalar1=-1.0, scalar2=1.0,
                                                            op0=ALU.mult, op1=ALU.add), r=["tf", "tlf"], w=["tf"])
                    P.add("dve", lambda e: e.tensor_tensor_scan(out=tb, data0=cv[:, CV_RST:CV_RST + 512], data1=tlf,
                                                                 initial=0.0, op0=ALU.mult, op1=ALU.add),
                          r=["tlf", "cv"], w=["tb"])
                    P.add("dve", lambda e: e.tensor_tensor(out=b3(td1), in0=b3(tb),
                                                            in1=b3(tb)[:, :, 63:64].broadcast_to([128, 4, 128]),
                                                            op=ALU.subtract), r=["tb"], w=["td1"])
                    P.add("dve", lambda e: e.tensor_tensor(out=b3(td2), in0=b3(tb)[:, :, 127:128].broadcast_to([128, 4, 128]),
                                                            in1=b3(tb), op=ALU.subtract), r=["tb"], w=["td2"])
                    P.add("act", lambda e: e.activation(out=te1, in_=td1, func=AF.Exp), r=["td1"], w=["te1"])
                    P.add("act", lambda e: e.activation(out=td1, in_=td1, func=AF.Exp, scale=-1.0), r=["td1", "te1"], w=["td1"])
                    P.add("act", lambda e: e.activation(out=td2, in_=td2, func=AF.Exp), r=["td2"], w=["td2"])
                    P.add("act", lambda e: e.activation(out=teb, in_=tb, func=AF.Exp), r=["tb"], w=["teb"])
                    P.add("act", lambda e: e.activation(out=ebl, in_=b3(tb)[:, :, 127], func=AF.Exp), r=["tb"], w=["ebl"])
                    if pending[0] is not None:
                        pending[0][0]()
                    P.add("dve", lambda e: e.tensor_tensor(out=qt_, in0=tqs, in1=te1, op=ALU.mult), r=["tqs", "te1"], w=["qt"])
                    P.add("dve", lambda e: e.tensor_tensor(out=qh_, in0=tqs, in1=teb, op=ALU.mult), r=["tqs", "teb"], w=["qh"])
                    P.add("dve", lambda e: e.tensor_tensor(out=kt_, in0=tf, in1=td1, op=ALU.mult), r=["tf", "td1"], w=["kt"])
                    P.add("dve", lambda e: e.tensor_tensor(out=kh_, in0=tf, in1=td2, op=ALU.mult), r=["tf", "td2"], w=["kh"])
                    pb = ps[4].bitcast(BF16)

                    def ftr(e, pb=pb):
                        ins = None
                        for k in range(4):
                            ins = e.transpose(out=pb[:, k * 128:(k + 1) * 128], in_=vT_[:, k * 128:(k + 1) * 128], identity=identb)
                        for k in range(4):
                            ins = e.transpose(out=pb[:, (4 + k) * 128:(5 + k) * 128], in_=kh_[:, k * 128:(k + 1) * 128],
                                              identity=identb)
                        return ins
                    P.add("pe", ftr, r=["vT", "kh", "identb"], w=[("ps", 4)])
                    if nxt is not None:
                        emit_proj(nxt[0], nxt[1], [2])
                    P.add("dve", lambda e, pb=pb: e.tensor_copy(out=tok, in_=pb.rearrange("p (a b) -> p a b", a=8)),
                          r=[("ps", 4)], w=["tok"])
                    st = stepi[0] % 2
                    stepi[0] += 1
                    Sb4 = Sbs[st]
                    Sprev = Sbs[1 - st]

                    def fa(e):
                        ins = None
                        for ch in range(4):
                            e.matmul(ps[5][:, ch * 128 + 64:(ch + 1) * 128], lhsT=kt_[:, ch * 128:(ch + 1) * 128],
                                     rhs=qt_[:, ch * 128 + 64:(ch + 1) * 128], start=True, stop=True)
                            ins = e.matmul(ps[5][0:64, ch * 128:ch * 128 + 64], lhsT=kt_[:, ch * 128:ch * 128 + 64],
                                           rhs=qt_[:, ch * 128:ch * 128 + 64], start=True, stop=True)
                        return ins
                    P.add("pe", fa, r=["kt", "qt"], w=[("ps", 5)])
                    p5 = ps[5].rearrange("p (a b) -> p a b", a=4)
                    P.add("dve", lambda e, p5=p5: e.tensor_tensor(
                        out=A4[:, :, 64:128], in0=p5[:, :, 64:128],
                        in1=maskb[:, 64:128].unsqueeze(1).broadcast_to([128, 4, 64]), op=ALU.mult),
                        r=[("ps", 5), "maskb"], w=["A4"])
                    P.add("dve", lambda e, p5=p5: e.tensor_tensor(
                        out=A4[0:64, :, 0:64], in0=p5[0:64, :, 0:64],
                        in1=maskb[0:64, 0:64].unsqueeze(1).broadcast_to([64, 4, 64]), op=ALU.mult),
                        r=[("ps", 5), "maskb", "A4"], w=["A4"])

                    def fsn(e):
                        ins = None
                        for ch in range(4):
                            ins = e.matmul(ps[7][:, ch * 128:(ch + 1) * 128], lhsT=tok[:, 4 + ch, :], rhs=tok[:, ch, :],
                                           start=True, stop=True)
                        return ins
                    P.add("pe", fsn, r=["tok"], w=[("ps", 7)])
                    if nxt is not None:
                        emit_proj(nxt[0], nxt[1], [3])
                    for ch in range(4):
                        P.add("dve", lambda e, ch=ch: e.scalar_tensor_tensor(
                            out=S, in0=S, scalar=ebl[:, ch:ch + 1], in1=ps[7][:, ch * 128:(ch + 1) * 128],
                            op0=ALU.mult, op1=ALU.add), r=["S", "ebl", ("ps", 7)], w=["S"])
                        P.add("act", lambda e, ch=ch, Sb4=Sb4: e.activation(out=Sb4[:, ch, :], in_=S, func=AF.Copy),
                              r=["S"], w=[("Sb", st, ch)])
                    if pending[0] is not None:
                        pending[0][1]()
                        pending[0] = None
                    for ch in range(4):
                        if ch == 0:
                            lhs_s = Sbz if tt == 0 else Sprev[:, 3, :]
                            rk = ["Sbz"] if tt == 0 else [("Sb", 1 - st, 3)]
                        else:
                            lhs_s = Sb4[:, ch - 1, :]
                            rk = [("Sb", st, ch - 1)]

                        def fo(e, ch=ch, lhs_s=lhs_s):
                            o = ps[6][:, ch * 128:(ch + 1) * 128]
                            e.matmul(o, lhsT=lhs_s, rhs=qh_[:, ch * 128:(ch + 1) * 128], start=True, stop=False)
                            return e.matmul(o, lhsT=tok[:, ch, :], rhs=A4[:, ch, :], start=False, stop=True)
                        P.add("pe", fo, r=rk + ["qh", "tok", "A4"], w=[("ps", 6)])
                    def tail_act():
                        P.add("act", lambda e: e.activation(out=tsq, in_=ps[6], func=AF.Square), r=[("ps", 6)], w=["tsq"])
                        P.add("pe", lambda e: e.matmul(ps[4], lhsT=onesb, rhs=tsq, start=True, stop=True),
                              r=["tsq", "onesb"], w=[("ps", 4)])
                        P.add("act", lambda e: e.activation(out=trr, in_=ps[4], func=AF.Ln, bias=epst, scale=1.0 / 128.0),
                              r=[("ps", 4), "epst"], w=["trr"])
                        P.add("act", lambda e: e.activation(out=trr, in_=trr, func=AF.Exp, scale=-0.5), r=["trr"], w=["trr"])

                    def tail_dve(h=h, tt=tt, tgs=tgs, kpar=kpar):
                        P.add("dve", lambda e: e.scalar_tensor_tensor(
                            out=tosb, in0=ps[6], scalar=cv[:, CV_GN + l:CV_GN + l + 1], in1=trr, op0=ALU.mult, op1=ALU.mult),
                            r=[("ps", 6), "trr", "cv"], w=["tosb"])
                        P.add("dve", lambda e: e.tensor_tensor(
                            out=mT[:, h, tt * 512:(tt + 1) * 512], in0=tosb, in1=tgs, op=ALU.mult),
                            r=["tosb", ("tgs", kpar)], w=[("mT", h, tt)])
                    pending[0] = (tail_act, tail_dve)
            if pending[0] is not None:
                pending[0][0]()
                pending[0][1]()
                pending[0] = None

        def mixer_proj(l):
            P.barrier()
            hT, aT, mT = mixer_fixed()
            if "attn" not in mix_sub:
                P.add("dve", lambda e: e.memset(aT, 0.0), w=[("aT", hh) for hh in range(4)])
            if "hgrn" not in mix_sub:
                P.add("dve", lambda e: e.memset(mT, 0.0), w=[("mT", hh, tq) for hh in range(8) for tq in range(4)])
            uT = arena.bf16(NCH, 1024)
            wc = [arena.bf16(12, 128) for _ in range(2)]
            ga = [arena.f32(512) for _ in range(2)]
            gm = [arena.f32(512) for _ in range(2)]
            xt = [arena.f32(512) for _ in range(3)]
            w2d = win_d[l]
            wc_rr = RR([0, 1])
            s_rr = RR([0, 1])
            x_rr = RR([0, 1, 2])
            y_rr = RR([0, 1, 2, 3, 4, 5, 6, 7])
            bset_rr = RR([(0, 1, 2, 3), (4, 5, 6, 7)])
            bgc = CV_BG + l * 32
            for th in range(2):
                for m in range(NCH):
                    wci = wc_rr.next()
                    P.add("pool", lambda e, wci=wci, m=m: e.dma_start(
                        out=wc[wci][:, 0:4, :], in_=wpa_d[l][:, m * 128:(m + 1) * 128].rearrange("(c p) n -> p c n", p=128)),
                        w=[("wc", wci, 0)], dma="wc%d_0" % wci)
                    P.add("pool", lambda e, wci=wci, m=m: e.dma_start(
                        out=wc[wci][:, 4:12, :], in_=wpm_d[l][:, m * 128:(m + 1) * 128].rearrange("(c p) n -> p c n", p=128)),
                        w=[("wc", wci, 1)], dma="wc%d_1" % wci)
                    s = wa_rr.next()
                    load_w(wa[s][:, :, 0:128], wrows(w2d, OFF_GATE + m * 128, 128), ("wa", s, 0), "wa%d_0" % s)
                    load_w(wa[s][:, :, 128:256], wrows(w2d, OFF_GATE + D + m * 128, 128), ("wa", s, 1), "wa%d_1" % s)
                    for t2 in range(2):
                        tq = th * 2 + t2
                        tsl = slice(tq * 512, (tq + 1) * 512)
                        usl = slice(t2 * 512, (t2 + 1) * 512)
                        hk = [("hT", c, tq) for c in range(NCH)]
                        B0, B1, B2, B3 = bset_rr.next()
                        mm_group(ps[B0], [(wa[s][:, c, 0:128], hT[:, c, tsl]) for c in range(NCH)],
                                 r=[("wa", s, 0)] + hk, w=[("ps", B0)])
                        mm_group(ps[B1], [(wa[s][:, c, 128:256], hT[:, c, tsl]) for c in range(NCH)],
                                 r=[("wa", s, 1)] + hk, w=[("ps", B1)])
                        mm_group(ps[B2], [(wc[wci][:, hh, :], aT[:, hh, tsl]) for hh in range(4)],
                                 r=[("wc", wci, 0)] + [("aT", hh) for hh in range(4)], w=[("ps", B2)])
                        mm_group(ps[B3], [(wc[wci][:, 4 + hh, :], mT[:, hh, tsl]) for hh in range(8)],
                                 r=[("wc", wci, 1)] + [("mT", hh, tq) for hh in range(8)], w=[("ps", B3)])
                        si = s_rr.next()
                        P.add("act", lambda e, si=si, m=m, B0=B0: e.activation(out=ga[si], in_=ps[B0], func=AF.Sigmoid,
                                                                       bias=cv[:, bgc + m:bgc + m + 1]),
                              r=[("ps", B0), "cv"], w=[("ga", si)])
                        P.add("act", lambda e, si=si, m=m, B1=B1: e.activation(out=gm[si], in_=ps[B1], func=AF.Sigmoid,
                                                                       bias=cv[:, bgc + 16 + m:bgc + 16 + m + 1]),
                              r=[("ps", B1), "cv"], w=[("gm", si)])
                        P.add("dve", lambda e, si=si, B2=B2: e.tensor_tensor(out=ga[si], in0=ga[si], in1=ps[B2], op=ALU.mult),
                              r=[("ga", si), ("ps", B2)], w=[("ga", si)])
                        P.add("dve", lambda e, si=si, B3=B3: e.tensor_tensor(out=gm[si], in0=gm[si], in1=ps[B3], op=ALU.mult),
                              r=[("gm", si), ("ps", B3)], w=[("gm", si)])
                        P.add("dve", lambda e, si=si, m=m, usl=usl: e.tensor_tensor(
                            out=uT[:, m, usl], in0=ga[si], in1=gm[si], op=ALU.add),
                            r=[("ga", si), ("gm", si)], w=[("uT", m, t2)])
                for m2 in range(NCH // 2):
                    s = wa_rr.next()
                    load_w(wa[s][:, :, 0:256], wrows(wo_d[l], m2 * 256, 256), [("wa", s, 0), ("wa", s, 1)], "wa%d_0" % s)
                    for mm in range(2):
                        m = m2 * 2 + mm
                        for t2 in range(2):
                            tq = th * 2 + t2
                            tsl = slice(tq * 512, (tq + 1) * 512)
                            usl = slice(t2 * 512, (t2 + 1) * 512)
                            by = y_rr.next()
                            xi = x_rr.next()
                            P.add("sp", lambda e, xi=xi, m=m, tsl=tsl: e.dma_start(out=xt[xi], in_=xT_d[m, :, tsl]),
                                  r=[("xT", m, tq)], w=[("xtp", xi)], dma="xtp%d" % xi)
                            mm_group(ps[by], [(wa[s][:, c, mm * 128:(mm + 1) * 128], uT[:, c, usl]) for c in range(NCH)],
                                     r=[("wa", s, 0), ("wa", s, 1)] + [("uT", c, t2) for c in range(NCH)], w=[("ps", by)])
                            P.add("dve", lambda e, by=by, xi=xi: e.tensor_tensor(out=xt[xi], in0=ps[by], in1=xt[xi], op=ALU.add),
                                  r=[("ps", by), ("xtp", xi)], w=[("xtp", xi)])
                            P.add("sp", lambda e, xi=xi, m=m, tsl=tsl: e.dma_start(out=xT_d[m, :, tsl], in_=xt[xi]),
                                  r=[("xtp", xi)], w=[("xT", m, tq)], dma="xtps%d" % xi)

        def final_phase(with_norm=True):
            P.barrier()
            arena.reset()
            xts = [arena.f32(NCH, 512) for _ in range(2)]
            yts = [arena.f32(NCH, 512) for _ in range(2)]
            sq = [arena.bf16(512) for _ in range(2)]
            rstds = [arena.f32(512) for _ in range(2)]
            ost = [arena.f32(D) for _ in range(2)]
            sq_rr = RR([(sq[0], "sq0"), (sq[1], "sq1")])
            ss_rr = RR([6, 7])
            b_rr = RR([0, 1, 2, 3, 4, 5])
            o_rr = RR([0, 1])

            def load_x(tq):
                bi = tq % 2
                for q4 in range(4):
                    P.add("sp", lambda e, bi=bi, q4=q4, tq=tq: e.dma_start(
                        out=xts[bi][:, q4 * 4:(q4 + 1) * 4, :],
                        in_=xT_d[q4 * 4:(q4 + 1) * 4, :, tq * 512:(tq + 1) * 512].rearrange("c p t -> p c t")),
                        r=[("xT", c, tq) for c in range(q4 * 4, q4 * 4 + 4)],
                        w=[("xf%d" % bi, c) for c in range(q4 * 4, q4 * 4 + 4)], dma="xtn%d_%d" % (bi, q4))
            load_x(0)
            for tq in range(4):
                bi = tq % 2
                xt, yt, rstd = xts[bi], yts[bi], rstds[bi]
                if tq + 1 < 4:
                    load_x(tq + 1)
                if with_norm:
                    rms_rstd(xt, 512, rstd, sq_rr, "xf%d" % bi, ss_rr, D)
                    for c in range(NCH):
                        P.add("dve", lambda e, c=c, xt=xt, yt=yt, rstd=rstd: e.scalar_tensor_tensor(
                            out=yt[:, c, :], in0=xt[:, c, :], scalar=cv[:, CV_FIN + c:CV_FIN + c + 1], in1=rstd,
                            op0=ALU.mult, op1=ALU.mult), r=[("xf%d" % bi, c), ("rstd", 0), "cv"], w=[("yf%d" % bi, c)])
                    src, sk = yt, "yf%d" % bi
                else:
                    src, sk = xt, "xf%d" % bi
                for tb in range(4):
                    oi = o_rr.next()
                    for q in range(4):
                        b = b_rr.next()

                        def ft(e, tb=tb, q=q, b=b, src=src):
                            ins = None
                            for j in range(4):
                                c = q * 4 + j
                                ins = e.transpose(out=ps[b][:, j * 128:(j + 1) * 128],
                                                  in_=src[:, c, tb * 128:(tb + 1) * 128], identity=ident)
                            return ins
                        P.add("pe", ft, r=[(sk, q * 4 + j) for j in range(4)] + ["ident"], w=[("ps", b)])
                        eng = "act" if q % 2 == 0 else "dve"

                        def fc(e, oi=oi, q=q, b=b, eng=eng):
                            o = ost[oi][:, q * 512:(q + 1) * 512]
                            if eng == "act":
                                return e.activation(out=o, in_=ps[b], func=AF.Copy)
                            return e.tensor_copy(out=o, in_=ps[b])
                        P.add(eng, fc, r=[("ps", b)], w=[("ost", oi, q)])
                    t0 = tq * 512 + tb * 128
                    P.add("sp", lambda e, oi=oi, t0=t0: e.dma_start(out=out_d[t0:t0 + 128, :], in_=ost[oi]),
                          r=[("ost", oi, q) for q in range(4)], w=[("OUT", seqi, t0)], dma="ost%d" % oi)

        for l in range(n_layers):
            if "ffn1" in do:
                ffn_phase(l, 0)
            if "mix" in do:
                mixer_norm(l)
                if "attn" in mix_sub:
                    mixer_attn(l)
                if "hgrn" in mix_sub:
                    mixer_hgrn(l)
                if "proj" in mix_sub:
                    mixer_proj(l)
            if "ffn2" in do:
                ffn_phase(l, 1)
        final_phase(with_norm=("nofinal" not in do))

    for sq in range(nseq):
        pipeline(x_all[sq], out_all[sq], xT_all[sq], sq)
    P.ops.append(("sp", None, tuple(("OUT", sq_, t0) for sq_ in range(nseq) for t0 in range(0, T, 128)) + ("phase",), ("END",), None))
    P.emit(dummy)
    return nc, P, arena


def make_consts(inputs):
    cvm = np.zeros((128, NCV), np.float32)
    fn = np.asarray(inputs["ffn_norm"], np.float32)
    cvm[:, CV_FFN:CV_FFN + 128] = fn.reshape(DEPTH, 2, NCH, 128).transpose(3, 0, 1, 2).reshape(128, -1)
    mn = np.asarray(inputs["mix_norm"], np.float32)
    cvm[:, CV_MIX:CV_MIX + 64] = mn.reshape(DEPTH, NCH, 128).transpose(2, 0, 1).reshape(128, -1)
    fin = np.asarray(inputs["final_norm"], np.float32)
    cvm[:, CV_FIN:CV_FIN + 16] = fin.reshape(NCH, 128).T
    bg = np.asarray(inputs["b_gate"], np.float32)
    cvm[:, CV_BG:CV_BG + 128] = bg.reshape(DEPTH, 32, 128).transpose(2, 0, 1).reshape(128, -1)
    lb = np.asarray(inputs["hgrn_lb"], np.float32)
    cvm[:, CV_LB:CV_LB + 32] = lb.reshape(DEPTH, 8, 128).transpose(2, 0, 1).reshape(128, -1)
    gn = np.asarray(inputs["hgrn_norm"], np.float32)
    cvm[:, CV_GN:CV_GN + 4] = gn.T
    p = np.arange(128)[:, None]
    f = np.arange(128)[None, :]
    cvm[:, CV_MASK:CV_MASK + 128] = (p <= f)
    cvm[:, CV_MASK + 128:CV_MASK + 256] = (p >= f)
    rst = np.ones((128, 512), np.float32)
    rst[:, 0::128] = 0.0
    cvm[:, CV_RST:CV_RST + 512] = rst
    return cvm, np.eye(128, dtype=np.float32)


_CACHE = {}


def run(inputs, n_layers=DEPTH, do=("ffn1", "mix", "ffn2"), mix_sub=("attn", "hgrn", "proj"), n_cores=N_CORES, trace=False, nseq=NSEQ):
    key = (n_layers, tuple(do), tuple(mix_sub), nseq)
    if key not in _CACHE:
        _CACHE[key] = build(n_layers, do, mix_sub, nseq)
    nc = _CACHE[key][0]
    cvm, ident = make_consts(inputs)
    x = np.ascontiguousarray(np.asarray(inputs["x"], np.float32))
    shared = {
        "ffn_w_in": np.ascontiguousarray(np.asarray(inputs["ffn_w_in"][:n_layers], np.float32)),
        "ffn_w_out": np.ascontiguousarray(np.asarray(inputs["ffn_w_out"][:n_layers], np.float32)),
        "w_in": np.ascontiguousarray(np.asarray(inputs["w_in"][:n_layers], np.float32)),
        "w_pa": np.ascontiguousarray(np.asarray(inputs["w_proj_attn"][:n_layers], np.float32)),
        "w_pm": np.ascontiguousarray(np.asarray(inputs["w_proj_hgrn"][:n_layers], np.float32)),
        "w_o": np.ascontiguousarray(np.asarray(inputs["w_out"][:n_layers], np.float32)),
        "cvec": cvm,
        "ident": ident,
    }
    in_maps = []
    for b in range(n_cores):
        m = dict(shared)
        m["x"] = np.ascontiguousarray(x[b * nseq:(b + 1) * nseq])
        in_maps.append(m)
    res = run_bass_kernel_spmd(nc, in_maps, core_ids=list(range(n_cores)), trace=trace)
    out = np.concatenate([np.asarray(r["out"], np.float32) for r in res.results], axis=0)
    return out, res


def kernel(**inputs):
    out, _ = run(inputs)
    return out
```

```python
import math
import numpy as np
import ml_dtypes
import concourse.bass as bass
import concourse.mybir as mybir
from concourse.bass_utils import run_bass_kernel_spmd

F32 = mybir.dt.float32
BF16 = mybir.dt.bfloat16
AF = mybir.ActivationFunctionType
ALU = mybir.AluOpType

D = 2048
T = 2048
DFF = 5504
NCH = D // 128
NJ = DFF // 128
DEPTH = 4
INW = 12800
EPS = 1e-6
N_CORES = 8
NSEQ = 1

OFF_QA, OFF_KA, OFF_VA = 0, 1536, 3072
OFF_QH, OFF_FH, OFF_IH, OFF_GH = 4608, 5632, 6656, 7680
OFF_GATE = 8704

CV_FFN = 0
CV_MIX = 128
CV_FIN = 192
CV_BG = 208
CV_LB = 336
CV_GN = 368
CV_MASK = 372
CV_RST = 628
NCV = 1140

SAME_SYNC = True
SEM_EPOCH = 20000


class Prog:
    ENGS = ("pe", "act", "dve", "pool", "sp")

    def __init__(self, nc):
        self.nc = nc
        self.ops = []
        self.cc_slots = set()

    def add(self, eng, fn, r=(), w=(), dma=None, free=False, cc=False):
        r = tuple(r)
        if not free:
            r = r + ("phase",)
        if cc:
            self.cc_slots.add(dma)
        self.ops.append((eng, fn, r, tuple(w), dma))

    def barrier(self):
        self.ops.append(("pool", None, (), ("phase",), None))

    def emit(self, dummy):
        nc = self.nc
        ops = self.ops
        n = len(ops)
        last_w = {}
        readers = {}
        deps = [None] * n
        for i, (eng, fn, r, w, dma) in enumerate(ops):
            d = set()
            for k in r:
                j = last_w.get(k)
                if j is not None:
                    d.add(j)
            for k in w:
                j = last_w.get(k)
                if j is not None:
                    d.add(j)
                rd = readers.get(k)
                if rd:
                    d.update(rd.values())
            d.discard(i)
            deps[i] = d
            src = (eng, dma) if dma else eng
            for k in r:
                readers.setdefault(k, {})[src] = i
            for k in w:
                last_w[k] = i
                readers[k] = {}
        need = [False] * n
        fdeps = [None] * n
        for i in range(n):
            eng = ops[i][0]
            lst = []
            for j in deps[i]:
                ej, _, _, _, dj = ops[j]
                if dj is None and ej == eng:
                    if eng == "pe" or eng == "sp" or not SAME_SYNC:
                        continue
                need[j] = True
                lst.append(j)
            lst.sort()
            fdeps[i] = lst
        cnt = {e: 0 for e in self.ENGS}
        slot_cnt = {}
        sig = [None] * n
        for i in range(n):
            eng, fn, r, w, dma = ops[i]
            if dma:
                slot_cnt[dma] = slot_cnt.get(dma, 0) + 1
                sig[i] = (("dma", dma), (1 if dma in self.cc_slots else 16) * slot_cnt[dma])
            elif need[i]:
                c = cnt[eng]
                cnt[eng] = c + 1
                sig[i] = (("eng", eng, c // SEM_EPOCH), c % SEM_EPOCH + 1)
        sems = {}
        for s in sig:
            if s is not None and s[0] not in sems:
                sems[s[0]] = nc.alloc_semaphore("s_" + "_".join(str(t) for t in s[0]))
        self.n_sems = len(sems)
        by_eng = {e: [] for e in self.ENGS}
        for i in range(n):
            by_eng[ops[i][0]].append(i)
        waited = {}
        self.n_wait = 0

        def run(engname, e):
            for i in by_eng[engname]:
                eng, fn, r, w, dma = ops[i]
                for j in fdeps[i]:
                    sk, val = sig[j]
                    key = (engname, sk)
                    if waited.get(key, 0) >= val:
                        continue
                    if sk[0] == "eng":
                        later = [kk for kk in waited if kk[0] == engname and kk[1][0] == "eng"
                                 and kk[1][1] == sk[1] and kk[1][2] > sk[2]]
                        if later:
                            continue
                    waited[key] = val
                    e.wait_ge(sems[sk], val)
                    self.n_wait += 1
                if fn is None:
                    ins = e.nop()
                else:
                    ins = fn(e)
                if sig[i] is not None:
                    if dma in self.cc_slots:
                        ins.then_inc(sems[sig[i][0]])
                    else:
                        ins.then_inc(sems[sig[i][0]], 16 if dma else 1)

        with nc.Block() as block:
            @block.tensor
            def _(e):
                run("pe", e)

            @block.scalar
            def _(e):
                run("act", e)

            @block.vector
            def _(e):
                run("dve", e)

            @block.gpsimd
            def _(e):
                run("pool", e)

            @block.sync
            def _(e):
                run("sp", e)


class Arena:
    def __init__(self, ap, nwords):
        self.ap = ap
        self.n = nwords
        self.off = 0
        self.peak = 0

    def reset(self):
        self.off = 0

    def _take(self, nw):
        a = self.ap[:, self.off:self.off + nw]
        self.off += nw
        self.peak = max(self.peak, self.off)
        assert self.off <= self.n, ("arena overflow", self.off, self.n)
        return a

    def f32(self, *shape):
        n = int(np.prod(shape))
        a = self._take(n)
        if len(shape) == 2:
            a = a.rearrange("p (a b) -> p a b", a=shape[0])
        return a

    def bf16(self, *shape):
        n = int(np.prod(shape))
        a = self._take((n + 1) // 2).bitcast(BF16)
        if len(shape) == 2:
            a = a.rearrange("p (a b) -> p a b", a=shape[0])
        return a


class RR:
    def __init__(self, items):
        self.items = list(items)
        self.i = 0

    def next(self):
        v = self.items[self.i % len(self.items)]
        self.i += 1
        return v


def build(n_layers=DEPTH, do=("ffn1", "mix", "ffn2"), mix_sub=("attn", "hgrn", "proj"), nseq=1):
    nc = bass.Bass("TRN2", target_bir_lowering=False)
    P = Prog(nc)

    x_all = nc.dram_tensor("x", [nseq, T, D], F32, kind="ExternalInput").ap()
    WSPEC = {
        "ffn_w_in": ((n_layers, 2), D, 2 * DFF, 32),
        "ffn_w_out": ((n_layers, 2), DFF, D, 172),
        "w_in": ((n_layers,), D, INW, 32),
        "w_pa": ((n_layers,), 512, D, 64),
        "w_pm": ((n_layers,), 1024, D, 128),
        "w_o": ((n_layers,), D, D, 128),
    }
    fwi_d = nc.dram_tensor("ffn_w_in", [n_layers, 2, D, 2 * DFF], F32, kind="ExternalInput").ap()
    fwo_d = nc.dram_tensor("ffn_w_out", [n_layers, 2, DFF, D], F32, kind="ExternalInput").ap()
    win_d = nc.dram_tensor("w_in", [n_layers, D, INW], F32, kind="ExternalInput").ap()
    wpa_d = nc.dram_tensor("w_pa", [n_layers, 512, D], F32, kind="ExternalInput").ap()
    wpm_d = nc.dram_tensor("w_pm", [n_layers, 1024, D], F32, kind="ExternalInput").ap()
    wo_d = nc.dram_tensor("w_o", [n_layers, D, D], F32, kind="ExternalInput").ap()

    def wkeys(nm, idx, r0=None, r1=None):
        lead, rtot, ccols, rp = WSPEC[nm]
        npc = rtot // (8 * rp)
        if r0 is None:
            ks = range(npc)
        else:
            ks = range(r0 // (8 * rp), (r1 - 1) // (8 * rp) + 1)
        return [("wf", nm, idx, k) for k in ks]

    cv_d = nc.dram_tensor("cvec", [128, NCV], F32, kind="ExternalInput").ap()
    id_d = nc.dram_tensor("ident", [128, 128], F32, kind="ExternalInput").ap()
    out_all = nc.dram_tensor("out", [nseq, T, D], F32, kind="ExternalOutput").ap()
    xT_all = nc.dram_tensor("xT_scr", [nseq, NCH, 128, T], F32, kind="Internal").ap()

    def sb(name, shape, dt):
        return nc.alloc_sbuf_tensor(name, shape, dt).ap()

    cv = sb("cv", [128, NCV], F32)
    ident = sb("ident_sb", [128, 128], F32)
    identb = sb("identb", [128, 128], BF16)
    ones32 = sb("ones32", [128, 128], F32)
    onesb = sb("onesb", [128, 128], BF16)
    maskb = sb("maskb", [128, 256], BF16)
    epst = sb("epst", [128, 1], F32)
    lbt = sb("lbt", [128, 32], F32)
    omlt = sb("omlt", [128, 32], F32)
    lbtmp = sb("lbtmp", [128, 48], F32)
    dummy = sb("dmy0", [128, 1], F32)
    WA_N = 4
    wa = [sb("wa%d" % i, [128, 16, 256], BF16) for i in range(WA_N)]
    ARENA_WORDS = 42400
    arena = Arena(sb("arena", [128, ARENA_WORDS], F32), ARENA_WORDS)
    ps = [nc.alloc_psum_tensor("ps%d" % i, [128, 512], F32).ap() for i in range(8)]

    wa_rr = RR(range(WA_N))

    P.add("sp", lambda e: e.dma_start(out=cv, in_=cv_d), w=["cv"], dma="cv", free=True)
    P.add("sp", lambda e: e.dma_start(out=ident, in_=id_d), w=["ident"], dma="ident", free=True)
    P.add("dve", lambda e: e.memset(ones32, 1.0), w=["ones32"], free=True)
    P.add("dve", lambda e: e.memset(onesb, 1.0), w=["onesb"], free=True)
    P.add("dve", lambda e: e.memset(epst, EPS), w=["epst"], free=True)
    P.add("dve", lambda e: e.tensor_copy(out=maskb, in_=cv[:, CV_MASK:CV_MASK + 256]),
          r=["cv"], w=["maskb"], free=True)
    P.add("dve", lambda e: e.tensor_copy(out=identb, in_=ident), r=["ident"], w=["identb"], free=True)
    ex = lbtmp[:, 0:32]
    ssum = lbtmp[:, 32:40]
    rs = lbtmp[:, 40:48]
    P.add("act", lambda e: e.activation(out=ex, in_=cv[:, CV_LB:CV_LB + 32], func=AF.Exp),
          r=["cv"], w=["lb_ex"], free=True)
    P.add("dve", lambda e: e.tensor_tensor(out=ssum, in0=ex[:, 0:8], in1=ex[:, 8:16], op=ALU.add),
          r=["lb_ex"], w=["lb_s"], free=True)
    P.add("dve", lambda e: e.tensor_tensor(out=ssum, in0=ssum, in1=ex[:, 16:24], op=ALU.add),
          r=["lb_ex", "lb_s"], w=["lb_s"], free=True)
    P.add("dve", lambda e: e.tensor_tensor(out=ssum, in0=ssum, in1=ex[:, 24:32], op=ALU.add),
          r=["lb_ex", "lb_s"], w=["lb_s"], free=True)
    P.add("dve", lambda e: e.reciprocal(out=rs, in_=ssum), r=["lb_s"], w=["lb_r"], free=True)
    P.add("dve", lambda e: e.memset(lbt[:, 0:8], 0.0), w=["lbt0"], free=True)
    for l in range(1, DEPTH):
        def f(e, l=l):
            return e.tensor_tensor(out=lbt[:, 8 * l:8 * l + 8], in0=ex[:, 8 * l:8 * l + 8], in1=rs, op=ALU.mult)
        P.add("dve", f, r=["lb_ex", "lb_r"], w=["lbt%d" % l], free=True)
    for l in range(2, DEPTH):
        def f(e, l=l):
            return e.tensor_tensor(out=lbt[:, 8 * l:8 * l + 8], in0=lbt[:, 8 * l:8 * l + 8],
                                   in1=lbt[:, 8 * l - 8:8 * l], op=ALU.add)
        P.add("dve", f, r=["lbt%d" % l, "lbt%d" % (l - 1)], w=["lbt%d" % l], free=True)
    P.add("dve", lambda e: e.tensor_scalar(out=omlt, in0=lbt, scalar1=-1.0, scalar2=1.0,
                                            op0=ALU.mult, op1=ALU.add),
          r=["lbt%d" % l for l in range(DEPTH)], w=["omlt"], free=True)
    LBK = ["lbt%d" % l for l in range(DEPTH)] + ["omlt"]

    def load_w(slot_ap, src_ap, key, slotname, rk=()):
        keys = key if isinstance(key, list) else [key]
        P.add("pool", lambda e: e.dma_start(out=slot_ap, in_=src_ap), r=list(rk), w=keys, dma=slotname, free=True)

    def wrows(w2d, c0, ncols):
        return w2d[:, c0:c0 + ncols].rearrange("(c p) n -> p c n", p=128)

    def mm_group(out_ps, pairs, r, w):
        def fn(e):
            ins = None
            n = len(pairs)
            for idx, (l, rh) in enumerate(pairs):
                ins = e.matmul(out_ps, lhsT=l, rhs=rh, start=(idx == 0), stop=(idx == n - 1))
            return ins
        P.add("pe", fn, r=r, w=w)

    def rms_rstd(x_tile, ntok, rstd, sq_bufs, tagx, ss_banks, nfeat, after_tt=None):
        for tt in range(ntok // 512):
            b = ss_banks.next()
            for c in range(NCH):
                s = sq_bufs.next()
                sqt, sqk = s

                def f1(e, c=c, tt=tt, sqt=sqt):
                    return e.activation(out=sqt, in_=x_tile[:, c, tt * 512:(tt + 1) * 512], func=AF.Square)
                P.add("act", f1, r=[(tagx, c)], w=[sqk])

                def f2(e, c=c, sqt=sqt, b=b):
                    return e.matmul(ps[b], lhsT=onesb, rhs=sqt, start=(c == 0), stop=(c == NCH - 1))
                P.add("pe", f2, r=[sqk, "onesb"] + ([("ps", b)] if c > 0 else []), w=[("ps", b)])

            def f3(e, tt=tt, b=b):
                return e.activation(out=rstd[:, tt * 512:(tt + 1) * 512], in_=ps[b], func=AF.Sqrt,
                                    bias=epst, scale=1.0 / nfeat)
            P.add("act", f3, r=[("ps", b), "epst"], w=[("rstd", tt)])

            def f4(e, tt=tt):
                return e.reciprocal(out=rstd[:, tt * 512:(tt + 1) * 512], in_=rstd[:, tt * 512:(tt + 1) * 512])
            P.add("dve", f4, r=[("rstd", tt)], w=[("rstd", tt)])
            if after_tt is not None:
                after_tt(tt)

    def pipeline(x_d, out_d, xT_d, seqi):
        P.barrier()
        arena.reset()
        xin = [arena.f32(D) for _ in range(3)]
        xst = [arena.f32(NCH, 128) for _ in range(2)]
        bank_rr = RR(range(8))

        def load_tile(t16):
            si = t16 % 3
            P.add("sp", lambda e, si=si, t16=t16: e.dma_start(out=xin[si], in_=x_d[t16 * 128:(t16 + 1) * 128, :]),
                  w=[("xin", si)], dma="xin%d" % si)
        load_tile(0)
        load_tile(1)
        for t16 in range(T // 128):
            s = t16 % 3
            s2 = t16 % 2
            for q in range(4):
                b = bank_rr.next()

                def ft(e, s=s, q=q, b=b):
                    ins = None
                    for j in range(4):
                        c = q * 4 + j
                        ins = e.transpose(out=ps[b][:, j * 128:(j + 1) * 128], in_=xin[s][:, c * 128:(c + 1) * 128],
                                          identity=ident)
                    return ins
                P.add("pe", ft, r=[("xin", s), "ident"], w=[("ps", b)])
                eng = "act" if q % 2 == 0 else "dve"

                def fc(e, s2=s2, q=q, b=b, eng=eng):
                    o = xst[s2][:, q * 4:(q + 1) * 4, :]
                    i = ps[b].rearrange("p (a b) -> p a b", a=4)
                    if eng == "act":
                        return e.activation(out=o, in_=i, func=AF.Copy)
                    return e.tensor_copy(out=o, in_=i)
                P.add(eng, fc, r=[("ps", b)], w=[("xst", s2, q)])
            P.add("sp", lambda e, s2=s2, t16=t16: e.dma_start(
                out=xT_d[:, :, t16 * 128:(t16 + 1) * 128].rearrange("c p t -> p c t"), in_=xst[s2]),
                r=[("xst", s2, q) for q in range(4)], w=[("xT", c, t16 // 4) for c in range(NCH)], dma="xst%d" % s2)
            if t16 + 2 < T // 128:
                load_tile(t16 + 2)

        def ffn_phase(l, i):
            P.barrier()
            arena.reset()
            x32 = arena.f32(NCH, 1024)
            hT = arena.bf16(NCH, 1024)
            gT = [arena.bf16(4, 1024) for _ in range(2)]
            wb = [arena.bf16(4, 2048) for _ in range(2)]
            sq = [arena.bf16(512) for _ in range(2)]
            rstd = arena.f32(1024)
            sg = [arena.f32(512) for _ in range(2)]
            w_in = fwi_d[l, i]
            w_out = fwo_d[l, i]
            gcol = CV_FFN + (l * 2 + i) * NCH
            gu_rr = RR([(0, 1), (2, 3)])
            y_rr = RR([4, 5, 6, 7])
            ss_rr = RR([6, 7])
            sq_rr = RR([(sq[0], "sq0"), (sq[1], "sq1")])
            sg_rr = RR([0, 1])
            g_rr = RR([0, 1])
            wb_rr = RR([0, 1])
            groups = []
            j = 0
            while j < NJ:
                groups.append(list(range(j, min(j + 4, NJ))))
                j += 4
            for half in range(2):
                T0 = half * 1024
                for c in range(NCH):
                    P.add("act" if half == 1 else "sp", lambda e, c=c, T0=T0: e.dma_start(out=x32[:, c, :], in_=xT_d[c, :, T0:T0 + 1024]),
                          r=[("xT", c, 2 * half), ("xT", c, 2 * half + 1)], w=[("x32", c)], dma="x32_%d" % c)
                def emit_h(tt):
                    for c in range(NCH):
                        def fh(e, c=c, tt=tt):
                            return e.scalar_tensor_tensor(
                                out=hT[:, c, tt * 512:(tt + 1) * 512], in0=x32[:, c, tt * 512:(tt + 1) * 512],
                                scalar=cv[:, gcol + c:gcol + c + 1], in1=rstd[:, tt * 512:(tt + 1) * 512],
                                op0=ALU.mult, op1=ALU.mult)
                        P.add("dve", fh, r=[("x32", c), ("rstd", tt), "cv"], w=[("hT", c, tt)])
                rms_rstd(x32, 1024, rstd, sq_rr, "x32", ss_rr, D, after_tt=emit_h)
                for gi, grp in enumerate(groups):
                    gs = g_rr.next()
                    slots_ = {}
                    for jj, j in enumerate(grp):
                        s = wa_rr.next()
                        slots_[jj] = s
                        load_w(wa[s][:, :, 0:128], wrows(w_in, j * 128, 128), ("wa", s, 0), "wa%d_0" % s, wkeys("ffn_w_in", (l, i)))
                        load_w(wa[s][:, :, 128:256], wrows(w_in, DFF + j * 128, 128), ("wa", s, 1), "wa%d_1" % s, wkeys("ffn_w_in", (l, i)))
                    if gi == 0:
                        order = [(jj, tt) for tt in range(2) for jj in range(len(grp))]
                    else:
                        order = [(jj, tt) for jj in range(len(grp)) for tt in range(2)]
                    for jj, tt in order:
                        s = slots_[jj]
                        bg, bu = gu_rr.next()
                        hk = [("hT", c, tt) for c in range(NCH)]
                        mm_group(ps[bg], [(wa[s][:, c, 0:128], hT[:, c, tt * 512:(tt + 1) * 512]) for c in range(NCH)],
                                 r=[("wa", s, 0)] + hk, w=[("ps", bg)])
                        mm_group(ps[bu], [(wa[s][:, c, 128:256], hT[:, c, tt * 512:(tt + 1) * 512]) for c in range(NCH)],
                                 r=[("wa", s, 1)] + hk, w=[("ps", bu)])
                        si = sg_rr.next()
                        P.add("act", lambda e, si=si, bg=bg: e.activation(out=sg[si], in_=ps[bg], func=AF.Silu),
                              r=[("ps", bg)], w=[("sg", si)])
                        P.add("dve", lambda e, si=si, bu=bu, gs=gs, jj=jj, tt=tt: e.tensor_tensor(
                            out=gT[gs][:, jj, tt * 512:(tt + 1) * 512], in0=sg[si], in1=ps[bu], op=ALU.mult),
                            r=[("sg", si), ("ps", bu)], w=[("gT", gs, jj, tt)])
                    ws = wb_rr.next()
                    G = len(grp)
                    j0 = grp[0]
                    P.add("pool", lambda e, ws=ws, G=G, j0=j0: e.dma_start(
                        out=wb[ws][:, 0:G, :], in_=w_out[j0 * 128:(j0 + G) * 128, :].rearrange("(g p) n -> p g n", p=128)),
                        r=wkeys("ffn_w_out", (l, i), j0 * 128, (j0 + G) * 128), w=[("wb", ws)], dma="wb%d" % ws)
                    for m in range(NCH):
                        for tt in range(2):
                            by = y_rr.next()
                            mm_group(ps[by], [(wb[ws][:, jj, m * 128:(m + 1) * 128], gT[gs][:, jj, tt * 512:(tt + 1) * 512])
                                              for jj in range(G)],
                                     r=[("wb", ws)] + [("gT", gs, jj, tt) for jj in range(G)], w=[("ps", by)])
                            P.add("dve", lambda e, by=by, m=m, tt=tt: e.scalar_tensor_tensor(
                                out=x32[:, m, tt * 512:(tt + 1) * 512], in0=ps[by], scalar=0.5,
                                in1=x32[:, m, tt * 512:(tt + 1) * 512], op0=ALU.mult, op1=ALU.add),
                                r=[("ps", by), ("x32", m)], w=[("x32", m)])
                for c in range(NCH):
                    P.add("sp", lambda e, c=c, T0=T0: e.dma_start(out=xT_d[c, :, T0:T0 + 1024], in_=x32[:, c, :]),
                          r=[("x32", c)], w=[("xT", c, 2 * half), ("xT", c, 2 * half + 1)], dma="x32s_%d" % c)

        def mixer_fixed():
            arena.reset()
            hT = arena.bf16(NCH, T)
            aT = arena.bf16(4, T)
            mT = arena.bf16(8, T)
            return hT, aT, mT

        def mixer_norm(l):
            P.barrier()
            hT, aT, mT = mixer_fixed()
            TW = 256
            xts = [arena.f32(NCH, TW) for _ in range(2)]
            sq = [arena.bf16(TW) for _ in range(2)]
            rstds = [arena.f32(TW) for _ in range(2)]
            gcol = CV_MIX + l * NCH
            sq_rr = RR([0, 1])
            for ti in range(T // TW):
                bi = ti % 2
                xt = xts[bi]
                rstd = rstds[bi]
                tq = (ti * TW) // 512
                t0 = ti * TW
                b = 6 + bi
                for q4 in range(4):
                    P.add("sp", lambda e, t0=t0, xt=xt, q4=q4: e.dma_start(
                        out=xt[:, q4 * 4:(q4 + 1) * 4, :],
                        in_=xT_d[q4 * 4:(q4 + 1) * 4, :, t0:t0 + TW].rearrange("c p t -> p c t")),
                        r=[("xT", c, tq) for c in range(q4 * 4, q4 * 4 + 4)],
                        w=[("xtn", bi, c) for c in range(q4 * 4, q4 * 4 + 4)], dma="xtn%d_%d" % (bi, q4))
                for c in range(NCH):
                    si = sq_rr.next()
                    P.add("act", lambda e, c=c, si=si, xt=xt: e.activation(out=sq[si], in_=xt[:, c, :], func=AF.Square),
                          r=[("xtn", bi, c)], w=[("sqn", si)])
                    P.add("pe", lambda e, c=c, si=si, b=b: e.matmul(ps[b][:, 0:TW], lhsT=onesb, rhs=sq[si],
                                                                     start=(c == 0), stop=(c == NCH - 1)),
                          r=[("sqn", si), "onesb"] + ([("ps", b)] if c > 0 else []), w=[("ps", b)])
                P.add("act", lambda e, b=b, rstd=rstd: e.activation(out=rstd, in_=ps[b][:, 0:TW], func=AF.Sqrt,
                                                                     bias=epst, scale=1.0 / D),
                      r=[("ps", b), "epst"], w=[("rstdn", bi)])
                P.add("dve", lambda e, rstd=rstd: e.reciprocal(out=rstd, in_=rstd), r=[("rstdn", bi)], w=[("rstdn", bi)])
                for c in range(NCH):
                    P.add("dve", lambda e, c=c, t0=t0, xt=xt, rstd=rstd: e.scalar_tensor_tensor(
                        out=hT[:, c, t0:t0 + TW], in0=xt[:, c, :], scalar=cv[:, gcol + c:gcol + c + 1], in1=rstd,
                        op0=ALU.mult, op1=ALU.mult),
                        r=[("xtn", bi, c), ("rstdn", bi), "cv"], w=[("hT", c, tq)])

        def proj_cols(hT, w2d, col0, slot, half, banks, tts=range(4)):
            for tt in tts:
                b = banks[tt]
                mm_group(ps[b], [(wa[slot][:, c, half * 128:(half + 1) * 128], hT[:, c, tt * 512:(tt + 1) * 512])
                                 for c in range(NCH)],
                         r=[("wa", slot, half)] + [("hT", c, tt) for c in range(NCH)], w=[("ps", b)])

        def mixer_attn(l):
            P.barrier()
            hT, aT, mT = mixer_fixed()
            num = arena.f32(T)
            den = arena.f32(T)
            qkv = [[arena.bf16(T) for _ in range(3)] for _ in range(2)]
            vtok = [arena.bf16(16, 128) for _ in range(2)]
            E = [arena.bf16(256) for _ in range(8)]
            w2d = win_d[l]
            set_rr = RR([0, 1])
            e_rr = RR(range(8))
            scale = 1.0 / math.sqrt(128.0)
            evac_rr = RR(["act", "dve"])
            for h in range(4):
                for g, r_ in enumerate((1, 4, 16)):
                    L = T // r_
                    nbs = L // 128
                    st = set_rr.next()
                    qT, kT, vT = qkv[st]
                    s0 = wa_rr.next()
                    load_w(wa[s0][:, :, 0:128], wrows(w2d, OFF_QA + g * 512 + h * 128, 128), ("wa", s0, 0), "wa%d_0" % s0, wkeys("w_in", (l,)))
                    load_w(wa[s0][:, :, 128:256], wrows(w2d, OFF_KA + g * 512 + h * 128, 128), ("wa", s0, 1), "wa%d_1" % s0, wkeys("w_in", (l,)))
                    s1 = wa_rr.next()
                    load_w(wa[s1][:, :, 0:128], wrows(w2d, OFF_VA + g * 512 + h * 128, 128), ("wa", s1, 0), "wa%d_0" % s1, wkeys("w_in", (l,)))
                    for (slot, hf, dst, nm) in ((s0, 0, qT, "q"), (s0, 1, kT, "k"), (s1, 0, vT, "v")):
                        banks = [0, 1, 2, 3] if nm != "k" else [4, 5, 6, 7]
                        proj_cols(hT, w2d, 0, slot, hf, banks)
                        for tt in range(4):
                            eng = evac_rr.next()

                            def fe(e, dst=dst, tt=tt, b=banks[tt], eng=eng):
                                if eng == "act":
                                    return e.activation(out=dst[:, tt * 512:(tt + 1) * 512], in_=ps[b], func=AF.Copy)
                                return e.tensor_copy(out=dst[:, tt * 512:(tt + 1) * 512], in_=ps[b])
                            P.add(eng, fe, r=[("ps", banks[tt])], w=[(nm, st, tt)])
                    qv = qT.rearrange("p (m r) -> p r m", r=r_)
                    kv = kT.rearrange("p (m r) -> p r m", r=r_)
                    vv = vT.rearrange("p (m r) -> p r m", r=r_)
                    nv = num.rearrange("p (m r) -> p r m", r=r_)
                    dv = den.rearrange("p (m r) -> p r m", r=r_)
                    allq = [("q", st, tt) for tt in range(4)]
                    allk = [("k", st, tt) for tt in range(4)]
                    allv = [("v", st, tt) for tt in range(4)]
                    for half8 in range(2):
                        b = 0 + half8
                        pb = ps[b].bitcast(BF16)

                        def ftr(e, half8=half8, pb=pb, vv=vv, nbs=nbs):
                            ins = None
                            for k in range(8):
                                B = half8 * 8 + k
                                c_, n_ = B // nbs, B % nbs
                                ins = e.transpose(out=pb[:, k * 128:(k + 1) * 128],
                                                  in_=vv[:, c_, n_ * 128:(n_ + 1) * 128], identity=identb)
                            return ins
                        P.add("pe", ftr, r=allv + ["identb"], w=[("ps", b)])
                        P.add("dve", lambda e, half8=half8, pb=pb, st=st: e.tensor_copy(
                            out=vtok[st][:, half8 * 8:(half8 + 1) * 8, :], in_=pb.rearrange("p (a b) -> p a b", a=8)),
                            r=[("ps", b)], w=[("vtok", st, half8)])
                    Eof = {}
                    for B in range(16):
                        c_, n_ = B // nbs, B % nbs
                        wid = 256 if n_ + 1 < nbs else 128
                        b = 2 + (B % 2)
                        ei = e_rr.next()
                        Eof[B] = ei

                        def fs(e, c_=c_, n_=n_, wid=wid, b=b, kv=kv, qv=qv):
                            return e.matmul(ps[b][:, 0:wid], lhsT=kv[:, c_, n_ * 128:(n_ + 1) * 128],
                                            rhs=qv[:, c_, n_ * 128:n_ * 128 + wid], start=True, stop=True)
                        P.add("pe", fs, r=allq + allk, w=[("ps", b)])
                        P.add("act", lambda e, b=b, wid=wid, ei=ei: e.activation(
                            out=E[ei][:, 0:wid], in_=ps[b][:, 0:wid], func=AF.Exp, scale=scale),
                            r=[("ps", b)], w=[("E", ei)])
                        P.add("dve", lambda e, wid=wid, ei=ei: e.tensor_tensor(
                            out=E[ei][:, 0:wid], in0=E[ei][:, 0:wid], in1=maskb[:, 0:wid], op=ALU.mult),
                            r=[("E", ei), "maskb"], w=[("E", ei)])
                        if B % 4 == 3:
                            B0 = B - 3
                            bo, bd = 4 + ((B // 4) % 2) * 2, 5 + ((B // 4) % 2) * 2

                            def fo(e, B0=B0, bo=bo, bd=bd, nbs=nbs, st=st, Eof=dict(Eof)):
                                ins = None
                                for lhs_kind, bank in (("v", bo), ("1", bd)):
                                    for k in range(4):
                                        Bq = B0 + k
                                        nq = Bq % nbs
                                        o = ps[bank][:, k * 128:(k + 1) * 128]
                                        has_prev = nq > 0
                                        if has_prev:
                                            lp = vtok[st][:, Bq - 1, :] if lhs_kind == "v" else onesb
                                            ins = e.matmul(o, lhsT=lp, rhs=E[Eof[Bq - 1]][:, 128:256], start=True, stop=False)
                                        lc = vtok[st][:, Bq, :] if lhs_kind == "v" else onesb
                                        ins = e.matmul(o, lhsT=lc, rhs=E[Eof[Bq]][:, 0:128], start=(not has_prev), stop=True)
                                return ins
                            need_e = [("E", Eof[bb]) for bb in range(max(B0 - 1, 0), B + 1)]
                            P.add("pe", fo, r=need_e + [("vtok", st, 0), ("vtok", st, 1), "onesb"],
                                  w=[("ps", bo), ("ps", bd)])
                            if r_ == 1:
                                no = nv[:, 0, B0 * 128:(B0 + 4) * 128]
                                do_ = dv[:, 0, B0 * 128:(B0 + 4) * 128]
                                pso, psd = ps[bo], ps[bd]
                            elif r_ == 4:
                                no = nv[:, B0 // 4, :]
                                do_ = dv[:, B0 // 4, :]
                                pso, psd = ps[bo], ps[bd]
                            else:
                                no = nv[:, B0:B0 + 4, :]
                                do_ = dv[:, B0:B0 + 4, :]
                                pso = ps[bo].rearrange("p (a b) -> p a b", a=4)
                                psd = ps[bd].rearrange("p (a b) -> p a b", a=4)
                            if g == 0:
                                P.add("act", lambda e, no=no, pso=pso: e.activation(out=no, in_=pso, func=AF.Copy),
                                      r=[("ps", bo)], w=[("num", B0 // 4)])
                                P.add("dve", lambda e, do_=do_, psd=psd: e.tensor_copy(out=do_, in_=psd),
                                      r=[("ps", bd)], w=[("den", B0 // 4)])
                            else:
                                allnum = [("num", k) for k in range(4)]
                                allden = [("den", k) for k in range(4)]
                                P.add("dve", lambda e, no=no, pso=pso: e.tensor_tensor(out=no, in0=pso, in1=no, op=ALU.add),
                                      r=[("ps", bo)] + allnum, w=allnum)
                                P.add("dve", lambda e, do_=do_, psd=psd: e.tensor_tensor(out=do_, in0=psd, in1=do_, op=ALU.add),
                                      r=[("ps", bd)] + allden, w=allden)
                allnum = [("num", k) for k in range(4)]
                allden = [("den", k) for k in range(4)]
                P.add("dve", lambda e: e.reciprocal(out=den, in_=den), r=allden, w=allden)
                P.add("dve", lambda e, h=h: e.tensor_tensor(out=aT[:, h, :], in0=num, in1=den, op=ALU.mult),
                      r=allnum + allden, w=[("aT", h)])

        def mixer_hgrn(l):
            P.barrier()
            hT, aT, mT = mixer_fixed()
            w2d = win_d[l]
            tf = arena.f32(512)
            tlf = arena.f32(512)
            tb = arena.f32(512)
            td1 = arena.f32(512)
            td2 = arena.f32(512)
            te1 = arena.f32(512)
            teb = arena.f32(512)
            tqs = arena.f32(512)
            tgss = [arena.f32(512) for _ in range(2)]
            pending = [None]
            tosb = arena.f32(512)
            tsq = arena.bf16(512)
            trr = arena.f32(512)
            qt_ = arena.bf16(512)
            qh_ = arena.bf16(512)
            kt_ = arena.bf16(512)
            kh_ = arena.bf16(512)
            vT_ = arena.bf16(512)
            tok = arena.bf16(8, 128)
            A4 = arena.bf16(4, 128)
            S = arena.f32(128)
            Sbs = [arena.bf16(4, 128) for _ in range(2)]
            Sbz = arena.bf16(128)
            ebl = arena.f32(4)
            P.add("dve", lambda e: e.memset(A4, 0.0), w=["A4"])
            P.add("dve", lambda e: e.memset(Sbz, 0.0), w=["Sbz"])
            stepi = [0]
            lcol = l * 8
            a_rr = RR([0, 1])
            hslots = {}

            def emit_proj(hh, tt_, pieces):
                if hh not in hslots:
                    s0 = wa_rr.next()
                    load_w(wa[s0][:, :, 0:128], wrows(w2d, OFF_QH + hh * 128, 128), ("wa", s0, 0), "wa%d_0" % s0)
                    load_w(wa[s0][:, :, 128:256], wrows(w2d, OFF_FH + hh * 128, 128), ("wa", s0, 1), "wa%d_1" % s0)
                    s1 = wa_rr.next()
                    load_w(wa[s1][:, :, 0:128], wrows(w2d, OFF_IH + hh * 128, 128), ("wa", s1, 0), "wa%d_0" % s1)
                    load_w(wa[s1][:, :, 128:256], wrows(w2d, OFF_GH + hh * 128, 128), ("wa", s1, 1), "wa%d_1" % s1)
                    hslots[hh] = (s0, s1)
                s0, s1 = hslots[hh]
                for p_ in pieces:
                    slot, half = ((s0, 0), (s0, 1), (s1, 0), (s1, 1))[p_]
                    proj_cols(hT, w2d, 0, slot, half, {tt_: p_}, tts=[tt_])

            for h in range(8):
                P.add("dve", lambda e: e.memset(S, 0.0), w=["S"])
                lb_ap = lbt[:, lcol + h:lcol + h + 1]
                oml_ap = omlt[:, lcol + h:lcol + h + 1]
                for tt in range(4):
                    if (h, tt) == (0, 0):
                        emit_proj(0, 0, [0, 1, 2, 3])
                    nxt = (h, tt + 1) if tt < 3 else ((h + 1, 0) if h < 7 else None)
                    b3 = lambda a: a.rearrange("p (a b) -> p a b", a=4)
                    kpar = (h * 4 + tt) % 2
                    tgs = tgss[kpar]
                    P.add("act", lambda e: e.activation(out=tf, in_=ps[1], func=AF.Sigmoid), r=[("ps", 1)], w=["tf"])
                    P.add("act", lambda e: e.activation(out=tqs, in_=ps[0], func=AF.Silu), r=[("ps", 0)], w=["tqs"])
                    P.add("act", lambda e, tgs=tgs: e.activation(out=tgs, in_=ps[3], func=AF.Silu), r=[("ps", 3)], w=[("tgs", kpar)])
                    P.add("act", lambda e: e.activation(out=vT_, in_=ps[2], func=AF.Copy), r=[("ps", 2)], w=["vT"])
                    if nxt is not None:
                        emit_proj(nxt[0], nxt[1], [0, 1])
                    P.add("dve", lambda e, oml_ap=oml_ap, lb_ap=lb_ap: e.tensor_scalar(
                        out=tf, in0=tf, scalar1=oml_ap, scalar2=lb_ap, op0=ALU.mult, op1=ALU.add),
                        r=["tf"] + LBK, w=["tf"])
                    P.add("act", lambda e: e.activation(out=tlf, in_=tf, func=AF.Ln), r=["tf"], w=["tlf"])
                    P.add("dve", lambda e: e.tensor_scalar(out=tf, in0=tf, scalar1=-1.0, scalar2=1.0,
                                                            op0=ALU.mult, op1=ALU.add), r=["tf", "tlf"], w=["tf"])
                    P.add("dve", lambda e: e.tensor_tensor_scan(out=tb, data0=cv[:, CV_RST:CV_RST + 512], data1=tlf,
                                                                 initial=0.0, op0=ALU.mult, op1=ALU.add),
                          r=["tlf", "cv"], w=["tb"])
                    P.add("dve", lambda e: e.tensor_tensor(out=b3(td1), in0=b3(tb),
                                                            in1=b3(tb)[:, :, 63:64].broadcast_to([128, 4, 128]),
                                                            op=ALU.subtract), r=["tb"], w=["td1"])
                    P.add("dve", lambda e: e.tensor_tensor(out=b3(td2), in0=b3(tb)[:, :, 127:128].broadcast_to([128, 4, 128]),
                                                            in1=b3(tb), op=ALU.subtract), r=["tb"], w=["td2"])
                    P.add("act", lambda e: e.activation(out=te1, in_=td1, func=AF.Exp), r=["td1"], w=["te1"])
                    P.add("act", lambda e: e.activation(out=td1, in_=td1, func=AF.Exp, scale=-1.0), r=["td1", "te1"], w=["td1"])
                    P.add("act", lambda e: e.activation(out=td2, in_=td2, func=AF.Exp), r=["td2"], w=["td2"])
                    P.add("act", lambda e: e.activation(out=teb, in_=tb, func=AF.Exp), r=["tb"], w=["teb"])
                    P.add("act", lambda e: e.activation(out=ebl, in_=b3(tb)[:, :, 127], func=AF.Exp), r=["tb"], w=["ebl"])
                    if pending[0] is not None:
                        pending[0][0]()
                    P.add("dve", lambda e: e.tensor_tensor(out=qt_, in0=tqs, in1=te1, op=ALU.mult), r=["tqs", "te1"], w=["qt"])
                    P.add("dve", lambda e: e.tensor_tensor(out=qh_, in0=tqs, in1=teb, op=ALU.mult), r=["tqs", "teb"], w=["qh"])
                    P.add("dve", lambda e: e.tensor_tensor(out=kt_, in0=tf, in1=td1, op=ALU.mult), r=["tf", "td1"], w=["kt"])
                    P.add("dve", lambda e: e.tensor_tensor(out=kh_, in0=tf, in1=td2, op=ALU.mult), r=["tf", "td2"], w=["kh"])
                    pb = ps[4].bitcast(BF16)

                    def ftr(e, pb=pb):
                        ins = None
                        for k in range(4):
                            ins = e.transpose(out=pb[:, k * 128:(k + 1) * 128], in_=vT_[:, k * 128:(k + 1) * 128], identity=identb)
                        for k in range(4):
                            ins = e.transpose(out=pb[:, (4 + k) * 128:(5 + k) * 128], in_=kh_[:, k * 128:(k + 1) * 128],
                                              identity=identb)
                        return ins
                    P.add("pe", ftr, r=["vT", "kh", "identb"], w=[("ps", 4)])
                    if nxt is not None:
                        emit_proj(nxt[0], nxt[1], [2])
                    P.add("dve", lambda e, pb=pb: e.tensor_copy(out=tok, in_=pb.rearrange("p (a b) -> p a b", a=8)),
                          r=[("ps", 4)], w=["tok"])
                    st = stepi[0] % 2
                    stepi[0] += 1
                    Sb4 = Sbs[st]
                    Sprev = Sbs[1 - st]

                    def fa(e):
                        ins = None
                        for ch in range(4):
                            e.matmul(ps[5][:, ch * 128 + 64:(ch + 1) * 128], lhsT=kt_[:, ch * 128:(ch + 1) * 128],
                                     rhs=qt_[:, ch * 128 + 64:(ch + 1) * 128], start=True, stop=True)
                            ins = e.matmul(ps[5][0:64, ch * 128:ch * 128 + 64], lhsT=kt_[:, ch * 128:ch * 128 + 64],
                                           rhs=qt_[:, ch * 128:ch * 128 + 64], start=True, stop=True)
                        return ins
                    P.add("pe", fa, r=["kt", "qt"], w=[("ps", 5)])
                    p5 = ps[5].rearrange("p (a b) -> p a b", a=4)
                    P.add("dve", lambda e, p5=p5: e.tensor_tensor(
                        out=A4[:, :, 64:128], in0=p5[:, :, 64:128],
                        in1=maskb[:, 64:128].unsqueeze(1).broadcast_to([128, 4, 64]), op=ALU.mult),
                        r=[("ps", 5), "maskb"], w=["A4"])
                    P.add("dve", lambda e, p5=p5: e.tensor_tensor(
                        out=A4[0:64, :, 0:64], in0=p5[0:64, :, 0:64],
                        in1=maskb[0:64, 0:64].unsqueeze(1).broadcast_to([64, 4, 64]), op=ALU.mult),
                        r=[("ps", 5), "maskb", "A4"], w=["A4"])

                    def fsn(e):
                        ins = None
                        for ch in range(4):
                            ins = e.matmul(ps[7][:, ch * 128:(ch + 1) * 128], lhsT=tok[:, 4 + ch, :], rhs=tok[:, ch, :],
                                           start=True, stop=True)
                        return ins
                    P.add("pe", fsn, r=["tok"], w=[("ps", 7)])
                    if nxt is not None:
                        emit_proj(nxt[0], nxt[1], [3])
                    for ch in range(4):
                        P.add("dve", lambda e, ch=ch: e.scalar_tensor_tensor(
                            out=S, in0=S, scalar=ebl[:, ch:ch + 1], in1=ps[7][:, ch * 128:(ch + 1) * 128],
                            op0=ALU.mult, op1=ALU.add), r=["S", "ebl", ("ps", 7)], w=["S"])
                        P.add("act", lambda e, ch=ch, Sb4=Sb4: e.activation(out=Sb4[:, ch, :], in_=S, func=AF.Copy),
                              r=["S"], w=[("Sb", st, ch)])
                    if pending[0] is not None:
                        pending[0][1]()
                        pending[0] = None
                    for ch in range(4):
                        if ch == 0:
                            lhs_s = Sbz if tt == 0 else Sprev[:, 3, :]
                            rk = ["Sbz"] if tt == 0 else [("Sb", 1 - st, 3)]
                        else:
                            lhs_s = Sb4[:, ch - 1, :]
                            rk = [("Sb", st, ch - 1)]

                        def fo(e, ch=ch, lhs_s=lhs_s):
                            o = ps[6][:, ch * 128:(ch + 1) * 128]
                            e.matmul(o, lhsT=lhs_s, rhs=qh_[:, ch * 128:(ch + 1) * 128], start=True, stop=False)
                            return e.matmul(o, lhsT=tok[:, ch, :], rhs=A4[:, ch, :], start=False, stop=True)
                        P.add("pe", fo, r=rk + ["qh", "tok", "A4"], w=[("ps", 6)])
                    def tail_act():
                        P.add("act", lambda e: e.activation(out=tsq, in_=ps[6], func=AF.Square), r=[("ps", 6)], w=["tsq"])
                        P.add("pe", lambda e: e.matmul(ps[4], lhsT=onesb, rhs=tsq, start=True, stop=True),
                              r=["tsq", "onesb"], w=[("ps", 4)])
                        P.add("act", lambda e: e.activation(out=trr, in_=ps[4], func=AF.Ln, bias=epst, scale=1.0 / 128.0),
                              r=[("ps", 4), "epst"], w=["trr"])
                        P.add("act", lambda e: e.activation(out=trr, in_=trr, func=AF.Exp, scale=-0.5), r=["trr"], w=["trr"])

                    def tail_dve(h=h, tt=tt, tgs=tgs, kpar=kpar):
                        P.add("dve", lambda e: e.scalar_tensor_tensor(
                            out=tosb, in0=ps[6], scalar=cv[:, CV_GN + l:CV_GN + l + 1], in1=trr, op0=ALU.mult, op1=ALU.mult),
                            r=[("ps", 6), "trr", "cv"], w=["tosb"])
                        P.add("dve", lambda e: e.tensor_tensor(
                            out=mT[:, h, tt * 512:(tt + 1) * 512], in0=tosb, in1=tgs, op=ALU.mult),
                            r=["tosb", ("tgs", kpar)], w=[("mT", h, tt)])
                    pending[0] = (tail_act, tail_dve)
            if pending[0] is not None:
                pending[0][0]()
                pending[0][1]()
                pending[0] = None

        def mixer_proj(l):
            P.barrier()
            hT, aT, mT = mixer_fixed()
            if "attn" not in mix_sub:
                P.add("dve", lambda e: e.memset(aT, 0.0), w=[("aT", hh) for hh in range(4)])
            if "hgrn" not in mix_sub:
                P.add("dve", lambda e: e.memset(mT, 0.0), w=[("mT", hh, tq) for hh in range(8) for tq in range(4)])
            uT = arena.bf16(NCH, 1024)
            wc = [arena.bf16(12, 128) for _ in range(2)]
            ga = [arena.f32(512) for _ in range(2)]
            gm = [arena.f32(512) for _ in range(2)]
            xt = [arena.f32(512) for _ in range(3)]
            w2d = win_d[l]
            wc_rr = RR([0, 1])
            s_rr = RR([0, 1])
            x_rr = RR([0, 1, 2])
            y_rr = RR([0, 1, 2, 3, 4, 5, 6, 7])
            bset_rr = RR([(0, 1, 2, 3), (4, 5, 6, 7)])
            bgc = CV_BG + l * 32
            for th in range(2):
                for m in range(NCH):
                    wci = wc_rr.next()
                    P.add("pool", lambda e, wci=wci, m=m: e.dma_start(
                        out=wc[wci][:, 0:4, :], in_=wpa_d[l][:, m * 128:(m + 1) * 128].rearrange("(c p) n -> p c n", p=128)),
                        w=[("wc", wci, 0)], dma="wc%d_0" % wci)
                    P.add("pool", lambda e, wci=wci, m=m: e.dma_start(
                        out=wc[wci][:, 4:12, :], in_=wpm_d[l][:, m * 128:(m + 1) * 128].rearrange("(c p) n -> p c n", p=128)),
                        w=[("wc", wci, 1)], dma="wc%d_1" % wci)
                    s = wa_rr.next()
                    load_w(wa[s][:, :, 0:128], wrows(w2d, OFF_GATE + m * 128, 128), ("wa", s, 0), "wa%d_0" % s)
                    load_w(wa[s][:, :, 128:256], wrows(w2d, OFF_GATE + D + m * 128, 128), ("wa", s, 1), "wa%d_1" % s)
                    for t2 in range(2):
                        tq = th * 2 + t2
                        tsl = slice(tq * 512, (tq + 1) * 512)
                        usl = slice(t2 * 512, (t2 + 1) * 512)
                        hk = [("hT", c, tq) for c in range(NCH)]
                        B0, B1, B2, B3 = bset_rr.next()
                        mm_group(ps[B0], [(wa[s][:, c, 0:128], hT[:, c, tsl]) for c in range(NCH)],
                                 r=[("wa", s, 0)] + hk, w=[("ps", B0)])
                        mm_group(ps[B1], [(wa[s][:, c, 128:256], hT[:, c, tsl]) for c in range(NCH)],
                                 r=[("wa", s, 1)] + hk, w=[("ps", B1)])
                        mm_group(ps[B2], [(wc[wci][:, hh, :], aT[:, hh, tsl]) for hh in range(4)],
                                 r=[("wc", wci, 0)] + [("aT", hh) for hh in range(4)], w=[("ps", B2)])
                        mm_group(ps[B3], [(wc[wci][:, 4 + hh, :], mT[:, hh, tsl]) for hh in range(8)],
                                 r=[("wc", wci, 1)] + [("mT", hh, tq) for hh in range(8)], w=[("ps", B3)])
                        si = s_rr.next()
                        P.add("act", lambda e, si=si, m=m, B0=B0: e.activation(out=ga[si], in_=ps[B0], func=AF.Sigmoid,
                                                                       bias=cv[:, bgc + m:bgc + m + 1]),
                              r=[("ps", B0), "cv"], w=[("ga", si)])
                        P.add("act", lambda e, si=si, m=m, B1=B1: e.activation(out=gm[si], in_=ps[B1], func=AF.Sigmoid,
                                                                       bias=cv[:, bgc + 16 + m:bgc + 16 + m + 1]),
                              r=[("ps", B1), "cv"], w=[("gm", si)])
                        P.add("dve", lambda e, si=si, B2=B2: e.tensor_tensor(out=ga[si], in0=ga[si], in1=ps[B2], op=ALU.mult),
                              r=[("ga", si), ("ps", B2)], w=[("ga", si)])
                        P.add("dve", lambda e, si=si, B3=B3: e.tensor_tensor(out=gm[si], in0=gm[si], in1=ps[B3], op=ALU.mult),
                              r=[("gm", si), ("ps", B3)], w=[("gm", si)])
                        P.add("dve", lambda e, si=si, m=m, usl=usl: e.tensor_tensor(
                            out=uT[:, m, usl], in0=ga[si], in1=gm[si], op=ALU.add),
                            r=[("ga", si), ("gm", si)], w=[("uT", m, t2)])
                for m2 in range(NCH // 2):
                    s = wa_rr.next()
                    load_w(wa[s][:, :, 0:256], wrows(wo_d[l], m2 * 256, 256), [("wa", s, 0), ("wa", s, 1)], "wa%d_0" % s)
                    for mm in range(2):
                        m = m2 * 2 + mm
                        for t2 in range(2):
                            tq = th * 2 + t2
                            tsl = slice(tq * 512, (tq + 1) * 512)
                            usl = slice(t2 * 512, (t2 + 1) * 512)
                            by = y_rr.next()
                            xi = x_rr.next()
                            P.add("sp", lambda e, xi=xi, m=m, tsl=tsl: e.dma_start(out=xt[xi], in_=xT_d[m, :, tsl]),
                                  r=[("xT", m, tq)], w=[("xtp", xi)], dma="xtp%d" % xi)
                            mm_group(ps[by], [(wa[s][:, c, mm * 128:(mm + 1) * 128], uT[:, c, usl]) for c in range(NCH)],
                                     r=[("wa", s, 0), ("wa", s, 1)] + [("uT", c, t2) for c in range(NCH)], w=[("ps", by)])
                            P.add("dve", lambda e, by=by, xi=xi: e.tensor_tensor(out=xt[xi], in0=ps[by], in1=xt[xi], op=ALU.add),
                                  r=[("ps", by), ("xtp", xi)], w=[("xtp", xi)])
                            P.add("sp", lambda e, xi=xi, m=m, tsl=tsl: e.dma_start(out=xT_d[m, :, tsl], in_=xt[xi]),
                                  r=[("xtp", xi)], w=[("xT", m, tq)], dma="xtps%d" % xi)

        def final_phase(with_norm=True):
            P.barrier()
            arena.reset()
            xts = [arena.f32(NCH, 512) for _ in range(2)]
            yts = [arena.f32(NCH, 512) for _ in range(2)]
            sq = [arena.bf16(512) for _ in range(2)]
            rstds = [arena.f32(512) for _ in range(2)]
            ost = [arena.f32(D) for _ in range(2)]
            sq_rr = RR([(sq[0], "sq0"), (sq[1], "sq1")])
            ss_rr = RR([6, 7])
            b_rr = RR([0, 1, 2, 3, 4, 5])
            o_rr = RR([0, 1])

            def load_x(tq):
                bi = tq % 2
                for q4 in range(4):
                    P.add("sp", lambda e, bi=bi, q4=q4, tq=tq: e.dma_start(
                        out=xts[bi][:, q4 * 4:(q4 + 1) * 4, :],
                        in_=xT_d[q4 * 4:(q4 + 1) * 4, :, tq * 512:(tq + 1) * 512].rearrange("c p t -> p c t")),
                        r=[("xT", c, tq) for c in range(q4 * 4, q4 * 4 + 4)],
                        w=[("xf%d" % bi, c) for c in range(q4 * 4, q4 * 4 + 4)], dma="xtn%d_%d" % (bi, q4))
            load_x(0)
            for tq in range(4):
                bi = tq % 2
                xt, yt, rstd = xts[bi], yts[bi], rstds[bi]
                if tq + 1 < 4:
                    load_x(tq + 1)
                if with_norm:
                    rms_rstd(xt, 512, rstd, sq_rr, "xf%d" % bi, ss_rr, D)
                    for c in range(NCH):
                        P.add("dve", lambda e, c=c, xt=xt, yt=yt, rstd=rstd: e.scalar_tensor_tensor(
                            out=yt[:, c, :], in0=xt[:, c, :], scalar=cv[:, CV_FIN + c:CV_FIN + c + 1], in1=rstd,
                            op0=ALU.mult, op1=ALU.mult), r=[("xf%d" % bi, c), ("rstd", 0), "cv"], w=[("yf%d" % bi, c)])
                    src, sk = yt, "yf%d" % bi
                else:
                    src, sk = xt, "xf%d" % bi
                for tb in range(4):
                    oi = o_rr.next()
                    for q in range(4):
                        b = b_rr.next()

                        def ft(e, tb=tb, q=q, b=b, src=src):
                            ins = None
                            for j in range(4):
                                c = q * 4 + j
                                ins = e.transpose(out=ps[b][:, j * 128:(j + 1) * 128],
                                                  in_=src[:, c, tb * 128:(tb + 1) * 128], identity=ident)
                            return ins
                        P.add("pe", ft, r=[(sk, q * 4 + j) for j in range(4)] + ["ident"], w=[("ps", b)])
                        eng = "act" if q % 2 == 0 else "dve"

                        def fc(e, oi=oi, q=q, b=b, eng=eng):
                            o = ost[oi][:, q * 512:(q + 1) * 512]
                            if eng == "act":
                                return e.activation(out=o, in_=ps[b], func=AF.Copy)
                            return e.tensor_copy(out=o, in_=ps[b])
                        P.add(eng, fc, r=[("ps", b)], w=[("ost", oi, q)])
                    t0 = tq * 512 + tb * 128
                    P.add("sp", lambda e, oi=oi, t0=t0: e.dma_start(out=out_d[t0:t0 + 128, :], in_=ost[oi]),
                          r=[("ost", oi, q) for q in range(4)], w=[("OUT", seqi, t0)], dma="ost%d" % oi)

        for l in range(n_layers):
            if "ffn1" in do:
                ffn_phase(l, 0)
            if "mix" in do:
                mixer_norm(l)
                if "attn" in mix_sub:
                    mixer_attn(l)
                if "hgrn" in mix_sub:
                    mixer_hgrn(l)
                if "proj" in mix_sub:
                    mixer_proj(l)
            if "ffn2" in do:
                ffn_phase(l, 1)
        final_phase(with_norm=("nofinal" not in do))

    for sq in range(nseq):
        pipeline(x_all[sq], out_all[sq], xT_all[sq], sq)
    P.ops.append(("sp", None, tuple(("OUT", sq_, t0) for sq_ in range(nseq) for t0 in range(0, T, 128)) + ("phase",), ("END",), None))
    P.emit(dummy)
    return nc, P, arena


def make_consts(inputs):
    cvm = np.zeros((128, NCV), np.float32)
    fn = np.asarray(inputs["ffn_norm"], np.float32)
    cvm[:, CV_FFN:CV_FFN + 128] = fn.reshape(DEPTH, 2, NCH, 128).transpose(3, 0, 1, 2).reshape(128, -1)
    mn = np.asarray(inputs["mix_norm"], np.float32)
    cvm[:, CV_MIX:CV_MIX + 64] = mn.reshape(DEPTH, NCH, 128).transpose(2, 0, 1).reshape(128, -1)
    fin = np.asarray(inputs["final_norm"], np.float32)
    cvm[:, CV_FIN:CV_FIN + 16] = fin.reshape(NCH, 128).T
    bg = np.asarray(inputs["b_gate"], np.float32)
    cvm[:, CV_BG:CV_BG + 128] = bg.reshape(DEPTH, 32, 128).transpose(2, 0, 1).reshape(128, -1)
    lb = np.asarray(inputs["hgrn_lb"], np.float32)
    cvm[:, CV_LB:CV_LB + 32] = lb.reshape(DEPTH, 8, 128).transpose(2, 0, 1).reshape(128, -1)
    gn = np.asarray(inputs["hgrn_norm"], np.float32)
    cvm[:, CV_GN:CV_GN + 4] = gn.T
    p = np.arange(128)[:, None]
    f = np.arange(128)[None, :]
    cvm[:, CV_MASK:CV_MASK + 128] = (p <= f)
    cvm[:, CV_MASK + 128:CV_MASK + 256] = (p >= f)
    rst = np.ones((128, 512), np.float32)
    rst[:, 0::128] = 0.0
    cvm[:, CV_RST:CV_RST + 512] = rst
    return cvm, np.eye(128, dtype=np.float32)


_CACHE = {}


def run(inputs, n_layers=DEPTH, do=("ffn1", "mix", "ffn2"), mix_sub=("attn", "hgrn", "proj"), n_cores=N_CORES, trace=False, nseq=NSEQ):
    key = (n_layers, tuple(do), tuple(mix_sub), nseq)
    if key not in _CACHE:
        _CACHE[key] = build(n_layers, do, mix_sub, nseq)
    nc = _CACHE[key][0]
    cvm, ident = make_consts(inputs)
    x = np.ascontiguousarray(np.asarray(inputs["x"], np.float32))
    shared = {
        "ffn_w_in": np.ascontiguousarray(np.asarray(inputs["ffn_w_in"][:n_layers], np.float32)),
        "ffn_w_out": np.ascontiguousarray(np.asarray(inputs["ffn_w_out"][:n_layers], np.float32)),
        "w_in": np.ascontiguousarray(np.asarray(inputs["w_in"][:n_layers], np.float32)),
        "w_pa": np.ascontiguousarray(np.asarray(inputs["w_proj_attn"][:n_layers], np.float32)),
        "w_pm": np.ascontiguousarray(np.asarray(inputs["w_proj_hgrn"][:n_layers], np.float32)),
        "w_o": np.ascontiguousarray(np.asarray(inputs["w_out"][:n_layers], np.float32)),
        "cvec": cvm,
        "ident": ident,
    }
    in_maps = []
    for b in range(n_cores):
        m = dict(shared)
        m["x"] = np.ascontiguousarray(x[b * nseq:(b + 1) * nseq])
        in_maps.append(m)
    res = run_bass_kernel_spmd(nc, in_maps, core_ids=list(range(n_cores)), trace=trace)
    out = np.concatenate([np.asarray(r["out"], np.float32) for r in res.results], axis=0)
    return out, res


def kernel(**inputs):
    out, _ = run(inputs)
    return out
```
